# Optimizing a Trainium2 kernel written in Bass

```python
import math
import jax, jax.numpy as jnp
from jax import lax
import numpy as np

D_MODEL = 1024
BATCH = 32
SEQ = 2048
DEPTH = 1

MEM_LEN = 256
D_FF = 2816
CONV_CH = D_MODEL
CONV_WIDTH = 31
DIFF_HEADS = 8
DIFF_HEAD_DIM = 64
DIFF_WIDTH = DIFF_HEADS * 2 * DIFF_HEAD_DIM
MEM_HEADS = 4
MEM_HEAD_DIM = 256
MEM_WIDTH = MEM_HEADS * MEM_HEAD_DIM
N_BRANCH = 3
ROPE_THETA = 10000.0
Q_BLOCK = 128
EPS = 1e-6
IN_COLS = 2 * CONV_CH + 3 * DIFF_WIDTH + MEM_WIDTH + N_BRANCH * D_MODEL

kernel_name = "hybrid_conformer_diffattn_memory_block"


def rmsnorm(x, g):
    xf = x.astype(jnp.float32)
    y = xf * lax.rsqrt(jnp.mean(xf * xf, axis=-1, keepdims=True) + EPS)
    return (y * g.astype(jnp.float32)).astype(x.dtype)


def layernorm(x, g, b):
    xf = x.astype(jnp.float32)
    mu = jnp.mean(xf, axis=-1, keepdims=True)
    var = jnp.mean(jnp.square(xf - mu), axis=-1, keepdims=True)
    y = (xf - mu) * lax.rsqrt(var + EPS)
    return (y * g.astype(jnp.float32) + b.astype(jnp.float32)).astype(x.dtype)


def swiglu(h, w_gu, w_down):
    gate, up = jnp.split(h @ w_gu, 2, axis=-1)
    return (jax.nn.silu(gate) * up) @ w_down


def rope_tables(seq, dim, dtype):
    inv_freq = 1.0 / (ROPE_THETA ** (jnp.arange(0, dim, 2, dtype=jnp.float32) / dim))
    ang = jnp.arange(seq, dtype=jnp.float32)[:, None] * inv_freq[None, :]
    return jnp.cos(ang).astype(dtype), jnp.sin(ang).astype(dtype)


def apply_rope(x, cos, sin):
    half = x.shape[-1] // 2
    x1, x2 = x[..., :half], x[..., half:]
    c, s = cos[:, None, :], sin[:, None, :]
    return jnp.concatenate([x1 * c - x2 * s, x1 * s + x2 * c], axis=-1)


def conv_module(glu_in, conv_w, conv_b, ln_g, ln_b, w_out):
    a, b = jnp.split(glu_in, 2, axis=-1)
    u = a * jax.nn.sigmoid(b)
    pad = CONV_WIDTH // 2
    u = lax.conv_general_dilated(u, conv_w.astype(u.dtype), window_strides=(1,),
                                 padding=[(pad, pad)],
                                 dimension_numbers=('NWC', 'WIO', 'NWC'),
                                 feature_group_count=CONV_CH) + conv_b
    u = jax.nn.silu(layernorm(u, ln_g, ln_b))
    return u @ w_out


def diff_attention(q, k, v, lam):
    B, S = q.shape[0], q.shape[1]
    nb = S // Q_BLOCK
    qb = q.reshape(B, nb, Q_BLOCK, DIFF_HEADS, 2, DIFF_HEAD_DIM).transpose(1, 0, 2, 3, 4, 5)
    scale = DIFF_HEAD_DIM ** -0.5

    def block(qblk):
        s = jnp.einsum('bqhcd,bkhcd->bhcqk', qblk, k,
                       preferred_element_type=jnp.float32) * scale
        p = jax.nn.softmax(s, axis=-1)
        a = p[:, :, 0] - lam * p[:, :, 1]
        return jnp.einsum('bhqk,bkhe->bqhe', a.astype(v.dtype), v)

    o = lax.map(block, qb)
    return o.transpose(1, 0, 2, 3, 4).reshape(B, S, DIFF_HEADS, 2 * DIFF_HEAD_DIM)


def memory_attention(q, k, v):
    s = jnp.einsum('bqhd,bkhd->bhqk', q, k,
                   preferred_element_type=jnp.float32) * (MEM_HEAD_DIM ** -0.5)
    p = jax.nn.softmax(s, axis=-1)
    return jnp.einsum('bhqk,bkhd->bqhd', p.astype(v.dtype), v)


def setup_inputs(seed: int = 0) -> dict:
    key = jax.random.key(seed)
    ks = jax.random.split(key, 32)
    L, D = DEPTH, D_MODEL
    f32 = jnp.float32

    def w(k, shape, fan_in):
        return jax.random.normal(k, shape, f32) * (fan_in ** -0.5)

    def gain(k, shape):
        return 1.0 + 0.02 * jax.random.normal(k, shape, f32)

    def small(k, shape, s=0.02):
        return s * jax.random.normal(k, shape, f32)

    return {
        'x': jax.random.normal(ks[0], (BATCH, SEQ, D), f32),
        'mem': jax.random.normal(ks[1], (BATCH, MEM_LEN, D), f32),
        'ffn1_norm': gain(ks[2], (L, D)),
        'ffn1_w_gu': w(ks[3], (L, D, 2 * D_FF), D),
        'ffn1_w_down': w(ks[4], (L, D_FF, D), D_FF),
        'mix_norm': gain(ks[5], (L, D)),
        'mem_norm': gain(ks[6], (L, D)),
        'w_in': w(ks[7], (L, D, IN_COLS), D),
        'b_gate': small(ks[8], (L, N_BRANCH * D)),
        'conv_w': w(ks[9], (L, CONV_WIDTH, 1, CONV_CH), CONV_WIDTH),
        'conv_b': small(ks[10], (L, CONV_CH)),
        'conv_ln_g': gain(ks[11], (L, CONV_CH)),
        'conv_ln_b': small(ks[12], (L, CONV_CH)),
        'w_conv_out': w(ks[13], (L, CONV_CH, D), CONV_CH),
        'diff_q_norm': gain(ks[14], (L, DIFF_HEAD_DIM)),
        'diff_k_norm': gain(ks[15], (L, DIFF_HEAD_DIM)),
        'diff_lambda': small(ks[16], (L, 4, DIFF_HEAD_DIM), 0.1),
        'diff_subln': gain(ks[17], (L, 2 * DIFF_HEAD_DIM)),
        'w_diff_out': w(ks[18], (L, DIFF_WIDTH, D), DIFF_WIDTH),
        'w_mem_kv': w(ks[19], (L, D, 2 * MEM_WIDTH), D),
        'mem_q_norm': gain(ks[20], (L, MEM_HEAD_DIM)),
        'mem_k_norm': gain(ks[21], (L, MEM_HEAD_DIM)),
        'w_mem_out': w(ks[22], (L, MEM_WIDTH, D), MEM_WIDTH),
        'w_o': w(ks[23], (L, D, D), D),
        'ffn2_norm': gain(ks[24], (L, D)),
        'ffn2_w_gu': w(ks[25], (L, D, 2 * D_FF), D),
        'ffn2_w_down': w(ks[26], (L, D_FF, D), D_FF),
    }


def reference(x, mem, ffn1_norm, ffn1_w_gu, ffn1_w_down, mix_norm, mem_norm, w_in, b_gate,
              conv_w, conv_b, conv_ln_g, conv_ln_b, w_conv_out, diff_q_norm, diff_k_norm,
              diff_lambda, diff_subln, w_diff_out, w_mem_kv, mem_q_norm, mem_k_norm,
              w_mem_out, w_o, ffn2_norm, ffn2_w_gu, ffn2_w_down):
    B, S, D = x.shape
    M = mem.shape[1]
    cos, sin = rope_tables(S, DIFF_HEAD_DIM, x.dtype)
    c1 = 2 * CONV_CH
    c2 = c1 + DIFF_WIDTH
    c3 = c2 + DIFF_WIDTH
    c4 = c3 + DIFF_WIDTH
    c5 = c4 + MEM_WIDTH
    for l in range(DEPTH):
        lam_init = 0.8 - 0.6 * math.exp(-0.3 * l)
        x = x + 0.5 * swiglu(rmsnorm(x, ffn1_norm[l]), ffn1_w_gu[l], ffn1_w_down[l])

        h = rmsnorm(x, mix_norm[l])
        proj = h @ w_in[l]
        glu_in, dq, dk, dv, mq, gl = jnp.split(proj, [c1, c2, c3, c4, c5], axis=-1)

        y_conv = conv_module(glu_in, conv_w[l], conv_b[l], conv_ln_g[l], conv_ln_b[l],
                             w_conv_out[l])

        q = apply_rope(rmsnorm(dq.reshape(B, S, 2 * DIFF_HEADS, DIFF_HEAD_DIM), diff_q_norm[l]), cos, sin)
        k = apply_rope(rmsnorm(dk.reshape(B, S, 2 * DIFF_HEADS, DIFF_HEAD_DIM), diff_k_norm[l]), cos, sin)
        q = q.reshape(B, S, DIFF_HEADS, 2, DIFF_HEAD_DIM)
        k = k.reshape(B, S, DIFF_HEADS, 2, DIFF_HEAD_DIM)
        v = dv.reshape(B, S, DIFF_HEADS, 2 * DIFF_HEAD_DIM)
        lp = diff_lambda[l].astype(jnp.float32)
        lam = jnp.exp(jnp.sum(lp[0] * lp[1])) - jnp.exp(jnp.sum(lp[2] * lp[3])) + lam_init
        o = diff_attention(q, k, v, lam)
        o = rmsnorm(o, diff_subln[l]) * (1.0 - lam_init)
        y_diff = o.reshape(B, S, DIFF_WIDTH) @ w_diff_out[l]

        kv = rmsnorm(mem, mem_norm[l]) @ w_mem_kv[l]
        mk, mv = jnp.split(kv, 2, axis=-1)
        mk = rmsnorm(mk.reshape(B, M, MEM_HEADS, MEM_HEAD_DIM), mem_k_norm[l])
        mv = mv.reshape(B, M, MEM_HEADS, MEM_HEAD_DIM)
        mqh = rmsnorm(mq.reshape(B, S, MEM_HEADS, MEM_HEAD_DIM), mem_q_norm[l])
        y_mem = memory_attention(mqh, mk, mv).reshape(B, S, MEM_WIDTH) @ w_mem_out[l]

        g = jax.nn.sigmoid((gl + b_gate[l]).astype(jnp.float32)).astype(x.dtype)
        g = g.reshape(B, S, N_BRANCH, D)
        merged = g[:, :, 0] * y_conv + g[:, :, 1] * y_diff + g[:, :, 2] * y_mem
        x = x + merged @ w_o[l]

        x = x + 0.5 * swiglu(rmsnorm(x, ffn2_norm[l]), ffn2_w_gu[l], ffn2_w_down[l])
    return x
```

```python
import math
from contextlib import ExitStack

import numpy as np

import concourse.bass as bass
import concourse.mybir as mybir
from concourse.bass_utils import run_bass_kernel_spmd

F32 = mybir.dt.float32
BF16 = mybir.dt.bfloat16
AF = mybir.ActivationFunctionType
ALU = mybir.AluOpType
AX = mybir.AxisListType

D = 1024
DC = 8
FF = 2816
FC = 22
MEM = 256
CW = 31
PAD = 15
EPS = 1e-6
LAM_INIT = 0.2
NCORES = 8

P_NORMS = 0
P_BG = 32
P_CB = 56
P_LNG = 64
P_LNB = 72
P_QN = 80
P_KN = 81
P_SUB = 82
P_MQN = 83
P_MKN = 85
P_CW = 88
P_LAM = 336
NPRM = 592
V_FFN1 = 0
V_MIX = 8
V_FFN2 = 16
V_MEMN = 24
V_QN = 32
V_KN = 33
V_SUB = 34
V_MQN = 35
V_MKN = 37
V_NLAM = 39
V_TMP = 40
NDRV = 48

NDMA = 24
GEN = 16000


class Tile:
    __slots__ = ("w", "r")

    def __init__(self):
        self.w = None
        self.r = {}


class KB:
    def __init__(self, nc, stack):
        self.nc = nc
        self.stack = stack
        self.eng = {"pe": nc.tensor, "act": nc.scalar, "dve": nc.vector, "pool": nc.gpsimd, "sp": nc.sync}
        self.cnt = {e: 0 for e in self.eng}
        self.sems = {}
        self.seen = {e: {} for e in self.eng}
        self.dma_cnt = [0] * NDMA
        self.dma_rr = 0
        self.dma_rr_sw = 0
        self.nins = 0

    def sem(self, key):
        s = self.sems.get(key)
        if s is None:
            s = self.stack.enter_context(self.nc.semaphore(f"s_{key[0]}_{key[1]}"))
            self.sems[key] = s
        return s

    @staticmethod
    def sigkey(eng, count):
        gen = (count - 1) // GEN
        return (eng, gen), count - gen * GEN

    def _collect(self, eng, reads, writes, is_dma):
        need = {}

        def add(kv, raw):
            key, val, peng = kv
            if not is_dma and peng == eng and eng == "pe":
                return
            if need.get(key, 0) < val:
                need[key] = val

        for t in reads:
            if t.w is not None:
                add(t.w, True)
        for t in writes:
            if t.w is not None:
                add(t.w, False)
            for kv in t.r.values():
                add(kv, False)
        return need

    def _waits(self, eng, need):
        seen = self.seen[eng]
        e = self.eng[eng]
        for key, val in need.items():
            if seen.get(key, 0) < val:
                e.wait_ge(self.sem(key), val)
                seen[key] = val
                self.nins += 1

    def _update(self, kv, reads, writes):
        key = kv[0]
        for t in reads:
            old = t.r.get(key)
            if old is None or old[1] < kv[1]:
                t.r[key] = kv
        for t in writes:
            t.w = kv
            t.r = {}

    def op(self, eng, fn, reads=(), writes=(), signal=True):
        need = self._collect(eng, reads, writes, False)
        self._waits(eng, need)
        ins = fn(self.eng[eng])
        self.nins += 1
        if signal:
            self.cnt[eng] += 1
            key, val = self.sigkey(eng, self.cnt[eng])
            ins.then_inc(self.sem(key), 1)
        else:
            key, val = self.sigkey(eng, self.cnt[eng] + 1)
        self._update((key, val, eng), reads, writes)
        return ins

    def dma(self, q, out, in_, reads=(), writes=()):
        half = NDMA // 2
        if q == "pool":
            i = self.dma_rr_sw
            self.dma_rr_sw = (i + 1) % half
        else:
            i = half + self.dma_rr
            self.dma_rr = (self.dma_rr + 1) % half
        need = self._collect(q, reads, writes, True)
        key = ("dma", i)
        if self.dma_cnt[i] > 0:
            need[key] = max(need.get(key, 0), self.dma_cnt[i])
        self._waits(q, need)
        ins = self.eng[q].dma_start(out=out, in_=in_)
        self.nins += 1
        self.dma_cnt[i] += 16
        ins.then_inc(self.sem(key), 16)
        self._update((key, self.dma_cnt[i], "dma"), reads, writes)
        return ins

    def barrier(self, engines=("pe", "act", "dve", "pool", "sp")):
        need = {}
        for e in ("pe", "act", "dve", "pool"):
            if self.cnt[e] > 0:
                k, v = self.sigkey(e, self.cnt[e])
                need[k] = v
        for i in range(NDMA):
            if self.dma_cnt[i] > 0:
                need[("dma", i)] = self.dma_cnt[i]
        for e in engines:
            self._waits(e, dict(need))

    def mm(self, ps, pst, lhsT, rhs, start, stop, reads, force=False):
        return self.op("pe", lambda e: e.matmul(ps, lhsT, rhs, start=start, stop=stop),
                       reads=reads, writes=[pst], signal=(stop or force))

    def act(self, out, in_, func, reads, writes, bias=None, scale=None):
        kw = {}
        if bias is not None:
            kw["bias"] = bias
        if scale is not None:
            kw["scale"] = scale
        return self.op("act", lambda e: e.activation(out=out, in_=in_, func=func, **kw), reads=reads, writes=writes)

    def tt(self, out, in0, in1, op, reads, writes):
        return self.op("dve", lambda e: e.tensor_tensor(out=out, in0=in0, in1=in1, op=op), reads=reads, writes=writes)

    def ts(self, out, in0, s1, s2, op0, op1, reads, writes):
        if op1 is None:
            return self.op("dve", lambda e: e.tensor_scalar(out=out, in0=in0, scalar1=s1, scalar2=None, op0=op0),
                           reads=reads, writes=writes)
        return self.op("dve", lambda e: e.tensor_scalar(out=out, in0=in0, scalar1=s1, scalar2=s2, op0=op0, op1=op1),
                       reads=reads, writes=writes)

    def rstd(self, out, in_, eps_n, reads, wt):
        self.act(out, in_, AF.Ln, reads, [wt], bias=float(eps_n))
        self.act(out, out, AF.Exp, [wt], [wt], scale=-0.5)

    def recip(self, out, in_, reads, wt):
        self.act(out, in_, AF.Ln, reads, [wt])
        self.act(out, out, AF.Exp, [wt], [wt], scale=-1.0)

    def stt(self, out, in0, scalar, in1, op0, op1, reads, writes):
        return self.op("dve", lambda e: e.scalar_tensor_tensor(out=out, in0=in0, scalar=scalar, in1=in1, op0=op0, op1=op1),
                       reads=reads, writes=writes)


class Ring:
    def __init__(self, items):
        self.items = items
        self.i = 0

    def next(self):
        it = self.items[self.i]
        self.i = (self.i + 1) % len(self.items)
        return it


def build_program(S, NB, stages=("ffn1", "conv", "att", "ffn2")):
    TT = 512
    NT = S // TT
    QT = 256
    NQ = S // QT
    NK = S // 128
    assert S % 1024 == 0
    SH = S // 2
    NTH = NT // 2

    nc = bass.Bass("TRN2", target_bir_lowering=False)
    dr = {}

    def din(name, shape):
        dr[name] = nc.dram_tensor(name, list(shape), F32, kind="ExternalInput").ap()
        return dr[name]

    xT_d = din("xT", [NB, 128, DC * S])
    memT_d = din("memT", [NB, 128, DC * MEM])
    prm_d = din("prm", [128, NPRM])
    cst_d = din("cst", [128, 512])
    rope_d = din("rope", [128, 2 * S])
    wgu_d = [din("wgu1", [FC, 128, 2 * DC * 128]), din("wgu2", [FC, 128, 2 * DC * 128])]
    wdn_d = [din("wdn1", [DC, 128, FC * 128]), din("wdn2", [DC, 128, FC * 128])]
    wab_d = din("wab", [DC, 128, 2 * DC * 128])
    wk_d = din("wk", [DC, 128, DC * 128])
    wq_d = din("wq", [DC, 128, DC * 128])
    wmq_d = din("wmq", [DC, 128, DC * 128])
    wv_d = din("wv", [4, 128, DC * 256])
    wkvv_d = din("wkvv", [4, 128, DC * 256])
    wkvk_d = din("wkvk", [DC, 128, DC * 128])
    wcg_d = din("wcg", [DC, 128, 2 * DC * 128])
    wdg_d = din("wdg", [DC, 128, 2 * DC * 128])
    wmg_d = din("wmg", [DC, 128, 2 * DC * 128])
    wo_d = din("wo", [DC, 128, DC * 128])
    out_d = nc.dram_tensor("outT", [NB, 128, DC * S], F32, kind="ExternalOutput").ap()
    hsp_d = nc.dram_tensor("hspill", [128, DC * S], BF16, kind="Internal").ap()
    hsp_v = hsp_d.rearrange("p (c s) -> p c s", c=DC)
    wc_src = {"wmq": (wmq_d, 1024), "wmg": (wmg_d, 2048), "wq": (wq_d, 1024), "wdg": (wdg_d, 2048),
              "wo": (wo_d, 1024), "wcg": (wcg_d, 2048), "wab": (wab_d, 2048)}
    wc_d = {nm: nc.dram_tensor("wc_" + nm, [DC, 128, n], BF16, kind="Internal").ap() for nm, (_, n) in wc_src.items()}

    stack = ExitStack()
    with stack:
        k = KB(nc, stack)

        def sb(name, shape, dt):
            return stack.enter_context(nc.sbuf_tensor(name, list(shape), dt))

        xT = sb("xT_sb", [128, DC, S], F32)
        xT_t = [[Tile() for _ in range(NT)] for _ in range(DC)]
        cst = sb("cst_sb", [128, 512], BF16)
        cst_t = Tile()
        ones = cst[:, 0:128]
        onesblk = cst[:, 128:256]
        pswap = cst[:, 256:384]
        ident = cst[:, 384:512]
        rope = sb("rope_sb", [128, 2 * S], BF16)
        rope_t = Tile()
        prm = sb("prm_sb", [128, P_LAM], F32)
        prm_t = Tile()
        drv = sb("drv", [128, NDRV], F32)
        drv_t = Tile()
        WSLOT = 3072
        wreg = sb("wreg", [128, 3 * WSLOT], BF16)
        wslots = Ring([(wreg[:, i * WSLOT:(i + 1) * WSLOT], Tile()) for i in range(3)])
        wslots9 = Ring([(wreg[:, i * 1024:(i + 1) * 1024], Tile()) for i in range(9)])
        sqr = Ring([(sb(f"sq{i}", [128, 512], BF16), Tile()) for i in range(2)])
        f32r = Ring([(sb(f"f32t{i}", [128, 512], F32), Tile()) for i in range(4)])
        lnm, lnm_t = sb("lnm", [128, 512], F32), Tile()
        lnv, lnv_t = sb("lnv", [128, 512], F32), Tile()
        hsp_t = [Tile() for _ in range(NT)]
        bfr = Ring([(sb(f"bft{i}", [128, 512], BF16), Tile()) for i in range(5)])
        AR_WORDS = 96 * 256
        arena = sb("arena", [128, AR_WORDS], F32)

        def carve(off_kib, n_elems, dt, pat=None, **kw):
            off = int(round(off_kib * 256))
            if dt == F32:
                ap = arena[:, off:off + n_elems]
            else:
                assert n_elems % 2 == 0
                ap = arena[:, off:off + n_elems // 2].bitcast(BF16)
            if pat:
                ap = ap.rearrange(pat, **kw)
            return ap

        psb = [(stack.enter_context(nc.psum_tensor(f"ps{i}", [128, 512], F32)), Tile()) for i in range(8)]
        ps_all = Ring(psb)

        k.dma("pool", cst[:], cst_d[:, :], writes=[cst_t])
        k.dma("pool", rope[:], rope_d[:, :], writes=[rope_t])
        k.dma("sp", prm[:], prm_d[:, 0:P_LAM], writes=[prm_t])
        k.ts(drv[:, 0:32], prm[:, 0:32], 32.0, None, ALU.mult, None, [prm_t], [drv_t])
        k.ts(drv[:, V_QN:V_QN + 2], prm[:, P_QN:P_QN + 2], 8.0, None, ALU.mult, None, [prm_t], [drv_t])
        k.ts(drv[:, V_SUB:V_SUB + 1], prm[:, P_SUB:P_SUB + 1], math.sqrt(128.0) * (1.0 - LAM_INIT), None, ALU.mult, None,
             [prm_t], [drv_t])
        k.ts(drv[:, V_MQN:V_MQN + 4], prm[:, P_MQN:P_MQN + 4], 16.0, None, ALU.mult, None, [prm_t], [drv_t])
        lp_ap, lp_t = f32r.next()
        k.dma("sp", lp_ap[:, 0:256], prm_d[:, P_LAM:P_LAM + 256], writes=[lp_t])
        lt_ap, lt_t = f32r.next()
        k.tt(lt_ap[:, 0:64], lp_ap[:, 0:64], lp_ap[:, 64:128], ALU.mult, [lp_t], [lt_t])
        k.tt(lt_ap[:, 64:128], lp_ap[:, 128:192], lp_ap[:, 192:256], ALU.mult, [lp_t], [lt_t])
        k.op("dve", lambda e: e.reduce_sum(out=drv[:, V_TMP:V_TMP + 1], in_=lt_ap[:, 0:64], axis=AX.X), [lt_t], [drv_t])
        k.op("dve", lambda e: e.reduce_sum(out=drv[:, V_TMP + 1:V_TMP + 2], in_=lt_ap[:, 64:128], axis=AX.X), [lt_t], [drv_t])
        k.act(drv[:, V_TMP + 2:V_TMP + 4], drv[:, V_TMP:V_TMP + 2], AF.Exp, [drv_t], [drv_t])
        k.tt(drv[:, V_TMP + 4:V_TMP + 5], drv[:, V_TMP + 3:V_TMP + 4], drv[:, V_TMP + 2:V_TMP + 3], ALU.subtract, [drv_t], [drv_t])
        k.ts(drv[:, V_NLAM:V_NLAM + 1], drv[:, V_TMP + 4:V_TMP + 5], -LAM_INIT, None, ALU.add, None, [drv_t], [drv_t])
        CONSTS = [cst_t, rope_t, prm_t, drv_t]

        def wload(src_ap, n_elems):
            ap, t = wslots.next()
            k.dma("pool", ap[:, 0:n_elems], src_ap, writes=[t])
            return ap, t

        wc_t = {nm: [Tile() for _ in range(DC)] for nm in wc_d}

        def wload_c(nm, j, piece=0):
            ap, t = wslots9.next()
            k.dma("sp", ap[:, 0:1024], wc_d[nm][j][:, piece * 1024:(piece + 1) * 1024], reads=[wc_t[nm][j]], writes=[t])
            return ap, t

        for nm, (src, n) in wc_src.items():
            for j in range(DC):
                ap, t = wload(src[j], n)
                k.dma("sp", wc_d[nm][j], ap[:, 0:n], reads=[t], writes=[wc_t[nm][j]])
        k.barrier()

        def rms_tile(src_fn, src_tiles, nch, ncol, eps_n, ps_pick):
            ps, pst = ps_pick()
            for c in range(nch):
                sq, sqt = sqr.next()
                k.act(sq[:, 0:ncol], src_fn(c), AF.Square, [src_tiles[c]], [sqt])
                k.mm(ps[:, 0:ncol], pst, ones, sq[:, 0:ncol], c == 0, c == nch - 1, [sqt, cst_t], force=True)
            rs, rst = f32r.next()
            k.rstd(rs[:, 0:ncol], ps[:, 0:ncol], eps_n, [pst], rst)
            return rs, rst

        def norm_full(hT, hT_t, gcol, spill):
            for t in range(NT):
                sl = slice(t * TT, (t + 1) * TT)
                rs, rst = rms_tile(lambda c: xT[:, c, sl], [xT_t[c][t] for c in range(DC)], DC, TT, D * EPS, ps_all.next)
                for c in range(DC):
                    k.stt(hT[:, c, sl], xT[:, c, sl], drv[:, gcol + c:gcol + c + 1], rs[:, 0:TT], ALU.mult, ALU.mult,
                          [xT_t[c][t], rst, drv_t], [hT_t[c][t]])
                if spill:
                    k.dma("sp", hsp_v[:, :, sl], hT[:, :, sl], reads=[hT_t[c][t] for c in range(DC)], writes=[hsp_t[t]])

        def ffn(idx, gcol):
            hT = carve(0, DC * S, BF16, "p (c t) -> p c t", c=DC)
            hT_t = [[Tile() for _ in range(NT)] for _ in range(DC)]
            actT = carve(32 * S / 2048, FC * SH, BF16, "p (c t) -> p c t", c=FC)
            act_t = [[Tile() for _ in range(NTH)] for _ in range(FC)]
            norm_full(hT, hT_t, gcol, False)
            for half in range(2):
                for j in range(FC):
                    w, wt = wload(wgu_d[idx][j], 2 * DC * 128)
                    wv = w[:, 0:2 * DC * 128].rearrange("p (g c m) -> p g c m", g=2, c=DC)
                    pg = [ps_all.next() for _ in range(NTH)]
                    pu = [ps_all.next() for _ in range(NTH)]
                    for g, pp in ((0, pg), (1, pu)):
                        for c in range(DC):
                            for tl in range(NTH):
                                t = half * NTH + tl
                                k.mm(pp[tl][0][:, :], pp[tl][1], wv[:, g, c, :], hT[:, c, t * TT:(t + 1) * TT],
                                     c == 0, c == DC - 1, [wt, hT_t[c][t]])
                    for tl in range(NTH):
                        sg, sgt = bfr.next()
                        k.act(sg[:, :], pg[tl][0][:, :], AF.Silu, [pg[tl][1]], [sgt])
                        k.tt(actT[:, j, tl * TT:(tl + 1) * TT], pu[tl][0][:, :], sg[:, :], ALU.mult,
                             [pu[tl][1], sgt], [act_t[j][tl]])
                for c in range(DC):
                    w, wt = wload(wdn_d[idx][c], FC * 128)
                    wv = w[:, 0:FC * 128].rearrange("p (f m) -> p f m", f=FC)
                    pd = [ps_all.next() for _ in range(NTH)]
                    for f in range(FC):
                        for tl in range(NTH):
                            k.mm(pd[tl][0][:, :], pd[tl][1], wv[:, f, :], actT[:, f, tl * TT:(tl + 1) * TT],
                                 f == 0, f == FC - 1, [wt, act_t[f][tl]])
                    for tl in range(NTH):
                        t = half * NTH + tl
                        sl = slice(t * TT, (t + 1) * TT)
                        k.stt(xT[:, c, sl], pd[tl][0][:, :], 0.5, xT[:, c, sl], ALU.mult, ALU.add,
                              [pd[tl][1], xT_t[c][t]], [xT_t[c][t]])
            k.barrier()

        def gated_proj(w_nm, jc, y_rhs_fn, y_reads_fn, h_rhs_fn, h_reads_fn, bcol, ncol, emit):
            wg, wgt = wload_c(w_nm, jc, 1)
            wy, wyt = wload_c(w_nm, jc, 0)
            wgv = wg[:, 0:DC * 128].rearrange("p (c m) -> p c m", c=DC)
            wyv = wy[:, 0:DC * 128].rearrange("p (c m) -> p c m", c=DC)
            py, pyt = ps_all.next()
            pg, pgt = ps_all.next()
            for c in range(DC):
                k.mm(pg[:, 0:ncol], pgt, wgv[:, c, :], h_rhs_fn(c), c == 0, c == DC - 1, [wgt] + h_reads_fn(c))
            for c in range(DC):
                k.mm(py[:, 0:ncol], pyt, wyv[:, c, :], y_rhs_fn(c), c == 0, c == DC - 1, [wyt] + y_reads_fn(c))
            g, gt = f32r.next()
            k.act(g[:, 0:ncol], pg[:, 0:ncol], AF.Sigmoid, [pgt, prm_t], [gt], bias=prm[:, P_BG + bcol:P_BG + bcol + 1])
            emit(py, pyt, g, gt)

        def wo_apply(m_fn, m_reads_fn, t0, ncol, xtiles_fn):
            for oc in range(DC):
                w, wt = wload_c("wo", oc)
                wv = w[:, 0:DC * 128].rearrange("p (c m) -> p c m", c=DC)
                po, pot = ps_all.next()
                for c in range(DC):
                    k.mm(po[:, 0:ncol], pot, wv[:, c, :], m_fn(c), c == 0, c == DC - 1, [wt] + m_reads_fn(c))
                xt = xtiles_fn(oc)
                k.tt(xT[:, oc, t0:t0 + ncol], po[:, 0:ncol], xT[:, oc, t0:t0 + ncol], ALU.add, [pot, xt], [xt])

        def memkv(b, mkT, mk_t, mv, mv_t):
            memT = carve(32, DC * MEM, F32, "p (c t) -> p c t", c=DC)
            memT_t = [Tile() for _ in range(DC)]
            memn = carve(40, DC * MEM, BF16, "p (c t) -> p c t", c=DC)
            memn_t = [Tile() for _ in range(DC)]
            for c in range(DC):
                k.dma("sp", memT[:, c, :], memT_d[b][:, c * MEM:(c + 1) * MEM], writes=[memT_t[c]])
            rs, rst = rms_tile(lambda c: memT[:, c, :], memT_t, DC, MEM, D * EPS, ps_all.next)
            for c in range(DC):
                k.stt(memn[:, c, :], memT[:, c, :], drv[:, V_MEMN + c:V_MEMN + c + 1], rs[:, 0:MEM], ALU.mult, ALU.mult,
                      [memT_t[c], rst, drv_t], [memn_t[c]])
            for hm in range(4):
                pk = []
                for i in range(2):
                    fc = 2 * hm + i
                    w, wt = wload(wkvk_d[fc], DC * 128)
                    wv = w[:, 0:DC * 128].rearrange("p (c m) -> p c m", c=DC)
                    p, pt = ps_all.next()
                    for c in range(DC):
                        k.mm(p[:, 0:MEM], pt, wv[:, c, :], memn[:, c, :], c == 0, c == DC - 1, [wt, memn_t[c]])
                    pk.append((p, pt))
                rs, rst = rms_tile(lambda i: pk[i][0][:, 0:MEM], [pk[0][1], pk[1][1]], 2, MEM, 256 * EPS, ps_all.next)
                for i in range(2):
                    fc = 2 * hm + i
                    k.stt(mkT[:, fc, :], pk[i][0][:, 0:MEM], drv[:, V_MKN + i:V_MKN + i + 1], rs[:, 0:MEM], ALU.mult, ALU.mult,
                          [pk[i][1], rst, drv_t], [mk_t[fc]])
            for qd in range(4):
                w, wt = wload(wkvv_d[qd], DC * 256)
                wv = w[:, 0:DC * 256].rearrange("p (c n) -> p c n", c=DC)
                for tk in range(2):
                    p, pt = ps_all.next()
                    for c in range(DC):
                        k.mm(p[:, 0:256], pt, memn[:, c, tk * 128:(tk + 1) * 128], wv[:, c, :], c == 0, c == DC - 1,
                             [wt, memn_t[c]])
                    k.act(mv[:, tk, qd * 256:(qd + 1) * 256], p[:, 0:256], AF.Copy, [pt], [mv_t[tk]])

        def qk_partA(px, pxt, gcol, ncol):
            y, yt = bfr.next()
            k.act(y[:, 0:ncol], px[:, 0:ncol], AF.Identity, [pxt, drv_t], [yt], scale=drv[:, gcol:gcol + 1])
            sq, sqt = sqr.next()
            k.act(sq[:, 0:ncol], px[:, 0:ncol], AF.Square, [pxt], [sqt])
            return (y, yt, sq, sqt)

        def qk_partB(stA, out_ap, out_t, t0, ncol):
            y, yt, sq, sqt = stA
            pss, psst = ps_all.next()
            k.mm(pss[:, 0:ncol], psst, onesblk, sq[:, 0:ncol], True, True, [sqt, cst_t])
            psp, pspt = ps_all.next()
            k.mm(psp[:, 0:ncol], pspt, pswap, y[:, 0:ncol], True, True, [yt, cst_t])
            rs, rst = f32r.next()
            k.rstd(rs[:, 0:ncol], pss[:, 0:ncol], 64 * EPS, [psst], rst)
            t1, t1t = f32r.next()
            k.tt(t1[:, 0:ncol], y[:, 0:ncol], rope[:, t0:t0 + ncol], ALU.mult, [yt, rope_t], [t1t])
            t2, t2t = f32r.next()
            k.tt(t2[:, 0:ncol], psp[:, 0:ncol], rope[:, S + t0:S + t0 + ncol], ALU.mult, [pspt, rope_t], [t2t])
            k.tt(t1[:, 0:ncol], t1[:, 0:ncol], t2[:, 0:ncol], ALU.add, [t1t, t2t], [t1t])
            if isinstance(out_ap, list):
                for prow, dst in out_ap:
                    k.tt(dst, t1[prow, 0:ncol], rs[prow, 0:ncol], ALU.mult, [t1t, rst], [out_t])
            else:
                k.tt(out_ap, t1[:, 0:ncol], rs[:, 0:ncol], ALU.mult, [t1t, rst], [out_t])

        def mixer(b):
            sc = S / 2048.0
            mkT = carve(88, DC * MEM, BF16, "p (c t) -> p c t", c=DC)
            mk_t = [Tile() for _ in range(DC)]
            mv = carve(92, 2 * D, BF16, "p (k n) -> p k n", k=2)
            mv_t = [Tile() for _ in range(2)]
            if "att" in stages:
                memkv(b, mkT, mk_t, mv, mv_t)
                k.barrier()
            hT = carve(0, DC * S, BF16, "p (c t) -> p c t", c=DC)
            hT_t = [[Tile() for _ in range(NT)] for _ in range(DC)]
            norm_full(hT, hT_t, V_MIX, True)

            if "conv" in stages:
                cT = carve(32 * sc, DC * S, BF16, "p (c t) -> p c t", c=DC)
                cT_t = [[Tile() for _ in range(NT)] for _ in range(DC)]
                mT = carve(64 * sc, DC * TT, BF16, "p (c t) -> p c t", c=DC)
                mT_t = [Tile() for _ in range(DC)]
                upad = carve(64 * sc + 8, S + 2 * PAD + 2, BF16)
                upad_t = Tile()
                dg = carve(64 * sc + 8 + (S + 32) * 2 / 1024.0 + 0.05, CW * 128, BF16, "p (j m) -> p j m", j=CW)
                dg_t = Tile()
                k.op("dve", lambda e: e.memset(upad[:, 0:S + 2 * PAD + 2], 0.0), [], [upad_t])
                for c in range(DC):
                    wpc = [wload_c("wab", c, g) for g in range(2)]
                    wvv = [w_[:, 0:DC * 128].rearrange("p (c m) -> p c m", c=DC) for w_, _ in wpc]
                    for j in range(CW):
                        k.ts(dg[:, j, :], ident, prm[:, P_CW + c * CW + j:P_CW + c * CW + j + 1], None, ALU.mult, None,
                             [cst_t, prm_t], [dg_t])
                    for t in range(NT):
                        sl = slice(t * TT, (t + 1) * TT)
                        pa, pat = ps_all.next()
                        pb, pbt = ps_all.next()
                        for g, (pp, ppt) in ((0, (pa, pat)), (1, (pb, pbt))):
                            for cc in range(DC):
                                k.mm(pp[:, :], ppt, wvv[g][:, cc, :], hT[:, cc, sl], cc == 0, cc == DC - 1, [wpc[g][1], hT_t[cc][t]])
                        sg, sgt = bfr.next()
                        k.act(sg[:, :], pb[:, :], AF.Sigmoid, [pbt], [sgt])
                        k.tt(upad[:, PAD + t * TT:PAD + (t + 1) * TT], pa[:, :], sg[:, :], ALU.mult, [pat, sgt], [upad_t])
                    for t in range(NT):
                        pc, pct = ps_all.next()
                        for j in range(CW):
                            k.mm(pc[:, :], pct, dg[:, j, :], upad[:, t * TT + j:t * TT + j + TT], j == 0, j == CW - 1,
                                 [dg_t, upad_t])
                        k.act(cT[:, c, t * TT:(t + 1) * TT], pc[:, :], AF.Identity, [pct, prm_t], [cT_t[c][t]],
                              bias=prm[:, P_CB + c:P_CB + c + 1])
                def conv_ln_stats(t):
                    sl = slice(t * TT, (t + 1) * TT)
                    pm, pmt = ps_all.next()
                    pq, pqt = ps_all.next()
                    for c in range(DC):
                        sq, sqt = sqr.next()
                        k.act(sq[:, :], cT[:, c, sl], AF.Square, [cT_t[c][t]], [sqt])
                        k.mm(pm[:, :], pmt, ones, cT[:, c, sl], c == 0, c == DC - 1, [cT_t[c][t], cst_t])
                        k.mm(pq[:, :], pqt, ones, sq[:, :], c == 0, c == DC - 1, [sqt, cst_t], force=True)
                    mean, meant = lnm, lnm_t
                    k.ts(mean[:, :], pm[:, :], 1.0 / D, None, ALU.mult, None, [pmt], [meant])
                    var, vart = lnv, lnv_t
                    k.tt(var[:, :], mean[:, :], mean[:, :], ALU.mult, [meant], [vart])
                    k.stt(var[:, :], pq[:, :], 1.0 / D, var[:, :], ALU.mult, ALU.subtract, [pqt, vart], [vart])
                    k.rstd(var[:, :], var[:, :], EPS, [vart], vart)
                    k.stt(mean[:, :], mean[:, :], -1.0, var[:, :], ALU.mult, ALU.mult, [meant, vart], [meant])

                def conv_ln_apply(t, c):
                    sl = slice(t * TT, (t + 1) * TT)
                    tmp, tmpt = f32r.next()
                    k.tt(tmp[:, :], cT[:, c, sl], lnv[:, :], ALU.mult, [cT_t[c][t], lnv_t], [tmpt])
                    k.tt(tmp[:, :], tmp[:, :], lnm[:, :], ALU.add, [tmpt, lnm_t], [tmpt])
                    k.act(cT[:, c, sl], tmp[:, :], AF.Silu, [tmpt, prm_t], [cT_t[c][t]],
                          bias=prm[:, P_LNB + c:P_LNB + c + 1], scale=prm[:, P_LNG + c:P_LNG + c + 1])

                def conv_proj(t, nxt):
                    sl = slice(t * TT, (t + 1) * TT)
                    for jc in range(DC):
                        def emit(py, pyt, g, gt, jc=jc):
                            k.tt(mT[:, jc, :], py[:, :], g[:, :], ALU.mult, [pyt, gt], [mT_t[jc]])
                        gated_proj("wcg", jc, lambda c: cT[:, c, sl], lambda c: [cT_t[c][t]],
                                   lambda c: hT[:, c, sl], lambda c: [hT_t[c][t]], jc, TT, emit)
                        if nxt is not None:
                            conv_ln_apply(nxt, jc)
                    wo_apply(lambda c: mT[:, c, :], lambda c: [mT_t[c]], t * TT, TT, lambda oc: xT_t[oc][t])

                conv_ln_stats(0)
                for c in range(DC):
                    conv_ln_apply(0, c)
                for t in range(NT):
                    nxt = t + 1 if t + 1 < NT else None
                    if nxt is not None:
                        conv_ln_stats(nxt)
                    conv_proj(t, nxt)
                k.barrier()

            if "att" in stages:
                k.barrier()
                kT = carve(0, DC * S, BF16, "p (c t) -> p c t", c=DC)
                kT_t = [[Tile() for _ in range(NT)] for _ in range(DC)]
                V = carve(32 * sc, NK * D, BF16, "p (k n) -> p k n", k=NK)
                V_t = [Tile() for _ in range(NK)]
                hbuf = [(carve(64 * sc + 8 * i, DC * TT, BF16, "p (c t) -> p c t", c=DC), Tile()) for i in range(2)]
                for t in range(NT):
                    hb, hbt = hbuf[t % 2]
                    k.dma("sp", hb[:, :, :], hsp_v[:, :, t * TT:(t + 1) * TT], reads=[hsp_t[t]], writes=[hbt])
                    prevB = None
                    for kc in range(DC):
                        w, wt = wload(wk_d[kc], DC * 128)
                        wv = w[:, 0:DC * 128].rearrange("p (c m) -> p c m", c=DC)
                        px, pxt = ps_all.next()
                        for c in range(DC):
                            k.mm(px[:, :], pxt, wv[:, c, :], hb[:, c, :], c == 0, c == DC - 1, [wt, hbt])
                        stA = qk_partA(px, pxt, V_KN, TT)
                        if prevB is not None:
                            qk_partB(*prevB)
                        prevB = (stA, kT[:, kc, t * TT:(t + 1) * TT], kT_t[kc][t], t * TT, TT)
                    qk_partB(*prevB)
                    for qd in range(4):
                        w, wt = wload(wv_d[qd], DC * 256)
                        wv = w[:, 0:DC * 256].rearrange("p (c n) -> p c n", c=DC)
                        for tk in range(4):
                            kk = t * 4 + tk
                            p, pt = ps_all.next()
                            for c in range(DC):
                                k.mm(p[:, 0:256], pt, hb[:, c, tk * 128:(tk + 1) * 128], wv[:, c, :], c == 0, c == DC - 1,
                                     [wt, hbt])
                            if (qd + tk) % 2 == 0:
                                k.act(V[:, kk, qd * 256:(qd + 1) * 256], p[:, 0:256], AF.Copy, [pt], [V_t[kk]])
                            else:
                                k.op("dve", lambda e, kk=kk, qd=qd, p=p: e.tensor_copy(out=V[:, kk, qd * 256:(qd + 1) * 256],
                                                                                        in_=p[:, 0:256]), [pt], [V_t[kk]])
                k.barrier()
                hq = [(carve(64 * sc + 4 * i, DC * QT, BF16, "p (c t) -> p c t", c=DC), Tile()) for i in range(2)]
                Qb = carve(64 * sc + 8, DC * 2 * QT, BF16, "p (c t) -> p c t", c=DC)
                Qb_t = [Tile() for _ in range(DC)]
                oT = carve(64 * sc + 16, DC * QT, BF16, "p (c t) -> p c t", c=DC)
                oT_t = [Tile() for _ in range(DC)]
                mg = carve(64 * sc + 20, DC * QT, BF16, "p (c t) -> p c t", c=DC)
                mg_t = [Tile() for _ in range(DC)]
                mqT, mq_t = mg, mg_t
                acc, acc_t = lnm, lnm_t
                ps_lo = Ring(psb[0:4])
                ps_hi = Ring(psb[6:8])
                for c in range(DC):
                    k.op("dve", lambda e, c=c: e.memset(Qb[:, c, :], 0.0), [], [Qb_t[c]])
                def hq_load(qj):
                    hbj, hbtj = hq[qj % 2]
                    k.dma("sp", hbj[:, :, :], hsp_v[:, :, qj * QT:(qj + 1) * QT], reads=[hsp_t[qj * QT // TT]], writes=[hbtj])

                for qi in range(NQ):
                    q0 = qi * QT
                    tq = q0 // TT
                    hb, hbt = hq[qi % 2]
                    if qi == 0:
                        hq_load(0)
                    mst = [dict() for _ in range(4)]

                    def m1a(hm):
                        pk = []
                        for i in range(2):
                            fc = 2 * hm + i
                            w, wt = wload_c("wmq", fc)
                            wv = w[:, 0:DC * 128].rearrange("p (c m) -> p c m", c=DC)
                            p, pt = ps_all.next()
                            for c in range(DC):
                                k.mm(p[:, 0:QT], pt, wv[:, c, :], hb[:, c, :], c == 0, c == DC - 1, [wt, hbt])
                            pk.append((p, pt))
                        mst[hm]["pk"] = pk

                    def m1b(hm):
                        pk = mst[hm]["pk"]
                        rs, rst = rms_tile(lambda i: pk[i][0][:, 0:QT], [pk[0][1], pk[1][1]], 2, QT, 256 * EPS, ps_all.next)
                        for i in range(2):
                            fc = 2 * hm + i
                            k.stt(mqT[:, fc, :], pk[i][0][:, 0:QT], drv[:, V_MQN + i:V_MQN + i + 1], rs[:, 0:QT],
                                  ALU.mult, ALU.mult, [pk[i][1], rst, drv_t], [mq_t[fc]])

                    def m2(hm):
                        psc, psct = ps_all.next()
                        for mc in range(2):
                            for i in range(2):
                                fc = 2 * hm + i
                                k.mm(psc[:, mc * QT:(mc + 1) * QT], psct, mkT[:, fc, mc * 128:(mc + 1) * 128], mqT[:, fc, :],
                                     i == 0, i == 1, [mk_t[fc], mq_t[fc]])
                        pm_, pmt_ = bfr.next()
                        k.act(pm_[:, :], psc[:, :], AF.Exp, [psct], [pmt_], scale=1.0 / 16.0)
                        mst[hm]["pm"] = (pm_, pmt_)

                    def m3(hm):
                        pm_, pmt_ = mst[hm]["pm"]
                        pso = [ps_all.next() for _ in range(2)]
                        pss, psst = ps_all.next()
                        for e2 in range(2):
                            for mc in range(2):
                                k.mm(pso[e2][0][:, 0:QT], pso[e2][1], mv[:, mc, hm * 256 + e2 * 128:hm * 256 + (e2 + 1) * 128],
                                     pm_[:, mc * QT:(mc + 1) * QT], mc == 0, mc == 1, [mv_t[mc], pmt_])
                        for mc in range(2):
                            k.mm(pss[:, 0:QT], psst, ones, pm_[:, mc * QT:(mc + 1) * QT], mc == 0, mc == 1, [pmt_, cst_t])
                        rr, rrt = f32r.next()
                        k.recip(rr[:, 0:QT], pss[:, 0:QT], [psst], rrt)
                        for e2 in range(2):
                            fc = 2 * hm + e2
                            k.tt(oT[:, fc, :], pso[e2][0][:, 0:QT], rr[:, 0:QT], ALU.mult, [pso[e2][1], rrt], [oT_t[fc]])

                    for fn, hm in ((m1a, 0), (m1a, 1), (m1b, 0), (m1a, 2), (m1b, 1), (m2, 0), (m1a, 3), (m1b, 2), (m2, 1),
                                   (m3, 0), (m1b, 3), (m2, 2), (m3, 1), (m2, 3), (m3, 2), (m3, 3)):
                        fn(hm)
                    for jc in range(DC):
                        def emit(py, pyt, g, gt, jc=jc):
                            k.tt(mg[:, jc, :], py[:, 0:QT], g[:, 0:QT], ALU.mult, [pyt, gt], [mg_t[jc]])
                        gated_proj("wmg", jc, lambda c: oT[:, c, :], lambda c: [oT_t[c]],
                                   lambda c: hb[:, c, :], lambda c: [hbt], 16 + jc, QT, emit)
                    prevB = None
                    for qc in range(DC):
                        w, wt = wload_c("wq", qc)
                        wv = w[:, 0:DC * 128].rearrange("p (c m) -> p c m", c=DC)
                        px, pxt = ps_all.next()
                        for c in range(DC):
                            k.mm(px[:, 0:QT], pxt, wv[:, c, :], hb[:, c, :], c == 0, c == DC - 1, [wt, hbt])
                        stA = qk_partA(px, pxt, V_QN, QT)
                        if prevB is not None:
                            qk_partB(*prevB)
                        prevB = (stA, [(slice(0, 64), Qb[0:64, qc, 0:QT]), (slice(64, 128), Qb[64:128, qc, QT:2 * QT])],
                                 Qb_t[qc], q0, QT)
                    qk_partB(*prevB)
                    def epilogue_stages(h, po, pot, acc, acc_t):
                        st = {}

                        def s1():
                            st["accb"] = bfr.next()
                            k.act(st["accb"][0][:, :], acc[:, :], AF.Copy, [acc_t], [st["accb"][1]])

                        def s1b():
                            st["psm"] = ps_hi.next()
                            k.mm(st["psm"][0][:, :], st["psm"][1], ones, st["accb"][0][:, :], True, True, [st["accb"][1], cst_t])

                        def s2():
                            st["rr"] = f32r.items[0]
                            k.recip(st["rr"][0][:, :], st["psm"][0][:, :], [st["psm"][1]], st["rr"][1])

                        def s3():
                            rr, rrt = st["rr"]
                            k.tt(rr[:, :], po[:, :], rr[:, :], ALU.mult, [pot, rrt], [rrt])
                            st["r0"] = f32r.items[1]
                            r0, r0t = st["r0"]
                            k.stt(r0[:, 0:QT], rr[:, QT:2 * QT], drv[:, V_NLAM:V_NLAM + 1], rr[:, 0:QT], ALU.mult, ALU.add,
                                  [rrt, drv_t], [r0t])

                        def s4():
                            r0, r0t = st["r0"]
                            st["sq"] = sqr.next()
                            k.act(st["sq"][0][:, 0:QT], r0[:, 0:QT], AF.Square, [r0t], [st["sq"][1]])

                        def s4b():
                            sq, sqt = st["sq"]
                            st["pn"] = ps_hi.next()
                            k.mm(st["pn"][0][:, 0:QT], st["pn"][1], ones, sq[:, 0:QT], True, True, [sqt, cst_t])

                        def s5():
                            r0, r0t = st["r0"]
                            k.rstd(r0[:, QT:2 * QT], st["pn"][0][:, 0:QT], 128 * EPS, [st["pn"][1]], r0t)

                        def s6():
                            r0, r0t = st["r0"]
                            k.stt(oT[:, h, :], r0[:, 0:QT], drv[:, V_SUB:V_SUB + 1], r0[:, QT:2 * QT], ALU.mult, ALU.mult,
                                  [r0t, drv_t], [oT_t[h]])

                        return [s1, s1b, s2, s3, s4, s4b, s5, s6]

                    if qi + 1 < NQ:
                        hq_load(qi + 1)
                    pending_epi = []
                    for h in range(DC):
                        po, pot = psb[4 + (h % 2)]
                        acc, acc_t = (lnm, lnm_t) if h % 2 == 0 else (lnv, lnv_t)
                        pend = []

                        def pv_step(kc, pe0, pet0):
                            k.mm(po[:, :], pot, V[:, kc, h * 128:(h + 1) * 128], pe0[:, :], kc == 0, kc == NK - 1, [V_t[kc], pet0])
                            if kc == 0:
                                k.op("dve", lambda e: e.tensor_copy(out=acc[:, :], in_=pe0[:, :]), [pet0], [acc_t])
                            else:
                                k.tt(acc[:, :], acc[:, :], pe0[:, :], ALU.add, [acc_t, pet0], [acc_t])

                        for kc in range(NK):
                            psc, psct = ps_lo.next()
                            k.mm(psc[:, :], psct, kT[:, h, kc * 128:(kc + 1) * 128], Qb[:, h, :], True, True,
                                 [kT_t[h][kc // 4], Qb_t[h]])
                            pe_, pet_ = bfr.next()
                            k.act(pe_[:, :], psc[:, :], AF.Exp, [psct], [pet_], scale=0.125)
                            pend.append((kc, pe_, pet_))
                            if len(pend) > 1:
                                pv_step(*pend.pop(0))
                            if pending_epi and kc >= 1 and (kc % 2 == 1 or NK < 16):
                                pending_epi.pop(0)()
                        pv_step(*pend.pop(0))
                        while pending_epi:
                            pending_epi.pop(0)()
                        pending_epi = epilogue_stages(h, po, pot, acc, acc_t)
                    while pending_epi:
                        pending_epi.pop(0)()
                    for jc in range(DC):
                        def emit(py, pyt, g, gt, jc=jc):
                            tmp, tmpt = f32r.next()
                            k.tt(tmp[:, 0:QT], py[:, 0:QT], g[:, 0:QT], ALU.mult, [pyt, gt], [tmpt])
                            k.tt(mg[:, jc, :], tmp[:, 0:QT], mg[:, jc, :], ALU.add, [tmpt, mg_t[jc]], [mg_t[jc]])
                        gated_proj("wdg", jc, lambda c: oT[:, c, :], lambda c: [oT_t[c]],
                                   lambda c: hb[:, c, :], lambda c: [hbt], 8 + jc, QT, emit)
                    wo_apply(lambda c: mg[:, c, :], lambda c: [mg_t[c]], q0, QT, lambda oc: xT_t[oc][tq])
                k.barrier()

        for b in range(NB):
            for c in range(DC):
                for t in range(NT):
                    k.dma("sp", xT[:, c, t * TT:(t + 1) * TT], xT_d[b][:, c * S + t * TT:c * S + (t + 1) * TT],
                          writes=[xT_t[c][t]])
            if "ffn1" in stages:
                ffn(0, V_FFN1)
            if "conv" in stages or "att" in stages:
                mixer(b)
            if "ffn2" in stages:
                ffn(1, V_FFN2)
            for c in range(DC):
                for t in range(NT):
                    k.dma("sp", out_d[b][:, c * S + t * TT:c * S + (t + 1) * TT], xT[:, c, t * TT:(t + 1) * TT],
                          reads=[xT_t[c][t]])
        k.barrier()
        build_program.last_nins = k.nins
    return nc


def _colblock(W, o, n=128):
    K = W.shape[0]
    return np.ascontiguousarray(W[:, o:o + n].reshape(K // 128, 128, n).transpose(1, 0, 2))


def _vec(v):
    return np.ascontiguousarray(v.reshape(-1, 128).T)


def rope_tables(S):
    half = 32
    inv_freq = (1.0 / (np.float32(10000.0) ** (np.arange(0, 64, 2, dtype=np.float32) / np.float32(64)))).astype(np.float32)
    ang = (np.arange(S, dtype=np.float32)[:, None] * inv_freq[None, :]).astype(np.float32)
    cos = np.cos(ang).astype(np.float32)
    sin = np.sin(ang).astype(np.float32)
    p = np.arange(128)
    C = cos[:, p % half].T
    sign = np.where((p % 64) < half, -1.0, 1.0).astype(np.float32)[:, None]
    Ss = sin[:, p % half].T * sign
    return np.ascontiguousarray(np.concatenate([C, Ss], axis=1).astype(np.float32))


def const_tables():
    p = np.arange(128)
    ones = np.ones((128, 128), np.float32)
    onesblk = (p[:, None] // 64 == p[None, :] // 64).astype(np.float32)
    sw = np.where((p % 64) < 32, p + 32, p - 32)
    pswap = (p[:, None] == sw[None, :]).astype(np.float32)
    ident = np.eye(128, dtype=np.float32)
    return np.ascontiguousarray(np.concatenate([ones, onesblk, pswap, ident], axis=1))


def prep_shared(inp, S):
    f = lambda a: np.asarray(a, dtype=np.float32)
    w_in = f(inp["w_in"])[0]
    c1, c2, c3, c4, c5 = 2048, 3072, 4096, 5120, 6144
    sh = {}
    prm = np.zeros((128, NPRM), np.float32)
    prm[:, 0:8] = _vec(f(inp["ffn1_norm"])[0])
    prm[:, 8:16] = _vec(f(inp["mix_norm"])[0])
    prm[:, 16:24] = _vec(f(inp["ffn2_norm"])[0])
    prm[:, 24:32] = _vec(f(inp["mem_norm"])[0])
    prm[:, P_BG:P_BG + 24] = _vec(f(inp["b_gate"])[0])
    prm[:, P_CB:P_CB + 8] = _vec(f(inp["conv_b"])[0])
    prm[:, P_LNG:P_LNG + 8] = _vec(f(inp["conv_ln_g"])[0])
    prm[:, P_LNB:P_LNB + 8] = _vec(f(inp["conv_ln_b"])[0])
    prm[:, P_QN] = np.tile(f(inp["diff_q_norm"])[0], 2)
    prm[:, P_KN] = np.tile(f(inp["diff_k_norm"])[0], 2)
    prm[:, P_SUB] = f(inp["diff_subln"])[0]
    prm[:, P_MQN:P_MQN + 2] = _vec(f(inp["mem_q_norm"])[0])
    prm[:, P_MKN:P_MKN + 2] = _vec(f(inp["mem_k_norm"])[0])
    cw = f(inp["conv_w"])[0][:, 0, :]
    prm[:, P_CW:P_CW + 248] = cw.T.reshape(DC, 128, CW).transpose(1, 0, 2).reshape(128, DC * CW)
    prm[:, P_LAM:P_LAM + 256] = np.broadcast_to(f(inp["diff_lambda"])[0].reshape(1, 256), (128, 256))
    sh["prm"] = prm
    sh["cst"] = const_tables()
    sh["rope"] = rope_tables(S)

    def blocks(W, offs, n=128):
        return np.ascontiguousarray(np.stack([_colblock(W, o, n).reshape(128, -1) for o in offs]))

    def blocks2(Wa, oa, Wb, ob):
        out = []
        for a, b_ in zip(oa, ob):
            out.append(np.concatenate([_colblock(Wa, a).reshape(128, -1), _colblock(Wb, b_).reshape(128, -1)], axis=1))
        return np.ascontiguousarray(np.stack(out))

    r8 = [i * 128 for i in range(8)]
    for i, nm in ((1, "ffn1"), (2, "ffn2")):
        wgu = f(inp[f"{nm}_w_gu"])[0]
        wdn = f(inp[f"{nm}_w_down"])[0]
        sh[f"wgu{i}"] = blocks2(wgu, [j * 128 for j in range(FC)], wgu, [FF + j * 128 for j in range(FC)])
        sh[f"wdn{i}"] = blocks(wdn, r8)
    sh["wab"] = blocks2(w_in, r8, w_in, [1024 + o for o in r8])
    sh["wq"] = blocks(w_in, [c1 + o for o in r8])
    sh["wk"] = blocks(w_in, [c2 + o for o in r8])
    sh["wv"] = blocks(w_in, [c3 + i * 256 for i in range(4)], 256)
    sh["wmq"] = blocks(w_in, [c4 + o for o in r8])
    wkv = f(inp["w_mem_kv"])[0]
    sh["wkvk"] = blocks(wkv, r8)
    sh["wkvv"] = blocks(wkv, [1024 + i * 256 for i in range(4)], 256)
    sh["wcg"] = blocks2(f(inp["w_conv_out"])[0], r8, w_in, [c5 + o for o in r8])
    sh["wdg"] = blocks2(f(inp["w_diff_out"])[0], r8, w_in, [c5 + 1024 + o for o in r8])
    sh["wmg"] = blocks2(f(inp["w_mem_out"])[0], r8, w_in, [c5 + 2048 + o for o in r8])
    sh["wo"] = blocks(f(inp["w_o"])[0], r8)
    return sh


def to_fm(x):
    nb, T, _ = x.shape
    return np.ascontiguousarray(x.reshape(nb, T, DC, 128).transpose(0, 3, 2, 1).reshape(nb, 128, DC * T))


def from_fm(y, T):
    nb = y.shape[0]
    return np.ascontiguousarray(y.reshape(nb, 128, DC, T).transpose(0, 3, 2, 1).reshape(nb, T, D))


_PROG_CACHE = {}


def run(inputs, S, NB, ncores, stages=("ffn1", "conv", "att", "ffn2")):
    x = np.asarray(inputs["x"], dtype=np.float32)
    mem = np.asarray(inputs["mem"], dtype=np.float32)
    sh = prep_shared(inputs, S)
    key = (S, NB, tuple(stages))
    if key not in _PROG_CACHE:
        _PROG_CACHE[key] = build_program(S, NB, stages)
    nc = _PROG_CACHE[key]
    in_maps = []
    for i in range(ncores):
        m = dict(sh)
        m["xT"] = to_fm(x[i * NB:(i + 1) * NB])
        m["memT"] = to_fm(mem[i * NB:(i + 1) * NB])
        in_maps.append(m)
    res = run_bass_kernel_spmd(nc, in_maps, core_ids=list(range(ncores)))
    outs = [from_fm(np.asarray(r["outT"]), S) for r in res.results]
    return np.concatenate(outs, axis=0)


def kernel(**inputs):
    B, S, _ = inputs["x"].shape
    NB = B // NCORES
    return run(inputs, S, NB, NCORES).astype(np.float32)
```

```python
import math
from contextlib import ExitStack

import numpy as np

import concourse.bass as bass
import concourse.mybir as mybir
from concourse.bass_utils import run_bass_kernel_spmd

F32 = mybir.dt.float32
BF16 = mybir.dt.bfloat16
AF = mybir.ActivationFunctionType
ALU = mybir.AluOpType
AX = mybir.AxisListType

D = 1024
DC = 8
FF = 2816
FC = 22
MEM = 256
CW = 31
PAD = 15
EPS = 1e-6
LAM_INIT = 0.2
NCORES = 8

P_NORMS = 0
P_BG = 32
P_CB = 56
P_LNG = 64
P_LNB = 72
P_QN = 80
P_KN = 81
P_SUB = 82
P_MQN = 83
P_MKN = 85
P_CW = 88
P_LAM = 336
NPRM = 592
V_FFN1 = 0
V_MIX = 8
V_FFN2 = 16
V_MEMN = 24
V_QN = 32
V_KN = 33
V_SUB = 34
V_MQN = 35
V_MKN = 37
V_NLAM = 39
V_TMP = 40
NDRV = 48

NDMA = 24
GEN = 16000


class Tile:
    __slots__ = ("w", "r")

    def __init__(self):
        self.w = None
        self.r = {}


class KB:
    def __init__(self, nc, stack):
        self.nc = nc
        self.stack = stack
        self.eng = {"pe": nc.tensor, "act": nc.scalar, "dve": nc.vector, "pool": nc.gpsimd, "sp": nc.sync}
        self.cnt = {e: 0 for e in self.eng}
        self.sems = {}
        self.seen = {e: {} for e in self.eng}
        self.dma_cnt = [0] * NDMA
        self.dma_rr = 0
        self.dma_rr_sw = 0
        self.nins = 0

    def sem(self, key):
        s = self.sems.get(key)
        if s is None:
            s = self.stack.enter_context(self.nc.semaphore(f"s_{key[0]}_{key[1]}"))
            self.sems[key] = s
        return s

    @staticmethod
    def sigkey(eng, count):
        gen = (count - 1) // GEN
        return (eng, gen), count - gen * GEN

    def _collect(self, eng, reads, writes, is_dma):
        need = {}

        def add(kv, raw):
            key, val, peng = kv
            if not is_dma and peng == eng and eng == "pe":
                return
            if need.get(key, 0) < val:
                need[key] = val

        for t in reads:
            if t.w is not None:
                add(t.w, True)
        for t in writes:
            if t.w is not None:
                add(t.w, False)
            for kv in t.r.values():
                add(kv, False)
        return need

    def _waits(self, eng, need):
        seen = self.seen[eng]
        e = self.eng[eng]
        for key, val in need.items():
            if seen.get(key, 0) < val:
                e.wait_ge(self.sem(key), val)
                seen[key] = val
                self.nins += 1

    def _update(self, kv, reads, writes):
        key = kv[0]
        for t in reads:
            old = t.r.get(key)
            if old is None or old[1] < kv[1]:
                t.r[key] = kv
        for t in writes:
            t.w = kv
            t.r = {}

    def op(self, eng, fn, reads=(), writes=(), signal=True):
        need = self._collect(eng, reads, writes, False)
        self._waits(eng, need)
        ins = fn(self.eng[eng])
        self.nins += 1
        if signal:
            self.cnt[eng] += 1
            key, val = self.sigkey(eng, self.cnt[eng])
            ins.then_inc(self.sem(key), 1)
        else:
            key, val = self.sigkey(eng, self.cnt[eng] + 1)
        self._update((key, val, eng), reads, writes)
        return ins

    def dma(self, q, out, in_, reads=(), writes=()):
        half = NDMA // 2
        if q == "pool":
            i = self.dma_rr_sw
            self.dma_rr_sw = (i + 1) % half
        else:
            i = half + self.dma_rr
            self.dma_rr = (self.dma_rr + 1) % half
        need = self._collect(q, reads, writes, True)
        key = ("dma", i)
        if self.dma_cnt[i] > 0:
            need[key] = max(need.get(key, 0), self.dma_cnt[i])
        self._waits(q, need)
        ins = self.eng[q].dma_start(out=out, in_=in_)
        self.nins += 1
        self.dma_cnt[i] += 16
        ins.then_inc(self.sem(key), 16)
        self._update((key, self.dma_cnt[i], "dma"), reads, writes)
        return ins

    def barrier(self, engines=("pe", "act", "dve", "pool", "sp")):
        need = {}
        for e in ("pe", "act", "dve", "pool"):
            if self.cnt[e] > 0:
                k, v = self.sigkey(e, self.cnt[e])
                need[k] = v
        for i in range(NDMA):
            if self.dma_cnt[i] > 0:
                need[("dma", i)] = self.dma_cnt[i]
        for e in engines:
            self._waits(e, dict(need))

    def mm(self, ps, pst, lhsT, rhs, start, stop, reads, force=False):
        return self.op("pe", lambda e: e.matmul(ps, lhsT, rhs, start=start, stop=stop),
                       reads=reads, writes=[pst], signal=(stop or force))

    def act(self, out, in_, func, reads, writes, bias=None, scale=None):
        kw = {}
        if bias is not None:
            kw["bias"] = bias
        if scale is not None:
            kw["scale"] = scale
        return self.op("act", lambda e: e.activation(out=out, in_=in_, func=func, **kw), reads=reads, writes=writes)

    def tt(self, out, in0, in1, op, reads, writes):
        return self.op("dve", lambda e: e.tensor_tensor(out=out, in0=in0, in1=in1, op=op), reads=reads, writes=writes)

    def ts(self, out, in0, s1, s2, op0, op1, reads, writes):
        if op1 is None:
            return self.op("dve", lambda e: e.tensor_scalar(out=out, in0=in0, scalar1=s1, scalar2=None, op0=op0),
                           reads=reads, writes=writes)
        return self.op("dve", lambda e: e.tensor_scalar(out=out, in0=in0, scalar1=s1, scalar2=s2, op0=op0, op1=op1),
                       reads=reads, writes=writes)

    def rstd(self, out, in_, eps_n, reads, wt):
        self.act(out, in_, AF.Ln, reads, [wt], bias=float(eps_n))
        self.act(out, out, AF.Exp, [wt], [wt], scale=-0.5)

    def recip(self, out, in_, reads, wt):
        self.act(out, in_, AF.Ln, reads, [wt])
        self.act(out, out, AF.Exp, [wt], [wt], scale=-1.0)

    def stt(self, out, in0, scalar, in1, op0, op1, reads, writes):
        return self.op("dve", lambda e: e.scalar_tensor_tensor(out=out, in0=in0, scalar=scalar, in1=in1, op0=op0, op1=op1),
                       reads=reads, writes=writes)


class Ring:
    def __init__(self, items):
        self.items = items
        self.i = 0

    def next(self):
        it = self.items[self.i]
        self.i = (self.i + 1) % len(self.items)
        return it


def build_program(S, NB, stages=("ffn1", "conv", "att", "ffn2")):
    TT = 512
    NT = S // TT
    QT = 256
    NQ = S // QT
    NK = S // 128
    assert S % 1024 == 0
    SH = S // 2
    NTH = NT // 2

    nc = bass.Bass("TRN2", target_bir_lowering=False)
    dr = {}

    def din(name, shape):
        dr[name] = nc.dram_tensor(name, list(shape), F32, kind="ExternalInput").ap()
        return dr[name]

    xT_d = din("xT", [NB, 128, DC * S])
    memT_d = din("memT", [NB, 128, DC * MEM])
    prm_d = din("prm", [128, NPRM])
    cst_d = din("cst", [128, 512])
    rope_d = din("rope", [128, 2 * S])
    wgu_d = [din("wgu1", [FC, 128, 2 * DC * 128]), din("wgu2", [FC, 128, 2 * DC * 128])]
    wdn_d = [din("wdn1", [DC, 128, FC * 128]), din("wdn2", [DC, 128, FC * 128])]
    wab_d = din("wab", [DC, 128, 2 * DC * 128])
    wk_d = din("wk", [DC, 128, DC * 128])
    wq_d = din("wq", [DC, 128, DC * 128])
    wmq_d = din("wmq", [DC, 128, DC * 128])
    wv_d = din("wv", [4, 128, DC * 256])
    wkvv_d = din("wkvv", [4, 128, DC * 256])
    wkvk_d = din("wkvk", [DC, 128, DC * 128])
    wcg_d = din("wcg", [DC, 128, 2 * DC * 128])
    wdg_d = din("wdg", [DC, 128, 2 * DC * 128])
    wmg_d = din("wmg", [DC, 128, 2 * DC * 128])
    wo_d = din("wo", [DC, 128, DC * 128])
    out_d = nc.dram_tensor("outT", [NB, 128, DC * S], F32, kind="ExternalOutput").ap()
    hsp_d = nc.dram_tensor("hspill", [128, DC * S], BF16, kind="Internal").ap()
    hsp_v = hsp_d.rearrange("p (c s) -> p c s", c=DC)
    wc_src = {"wmq": (wmq_d, 1024), "wmg": (wmg_d, 2048), "wq": (wq_d, 1024), "wdg": (wdg_d, 2048),
              "wo": (wo_d, 1024), "wcg": (wcg_d, 2048)}
    wc_d = {nm: nc.dram_tensor("wc_" + nm, [DC, 128, n], BF16, kind="Internal").ap() for nm, (_, n) in wc_src.items()}

    stack = ExitStack()
    with stack:
        k = KB(nc, stack)

        def sb(name, shape, dt):
            return stack.enter_context(nc.sbuf_tensor(name, list(shape), dt))

        xT = sb("xT_sb", [128, DC, S], F32)
        xT_t = [[Tile() for _ in range(NT)] for _ in range(DC)]
        cst = sb("cst_sb", [128, 512], BF16)
        cst_t = Tile()
        ones = cst[:, 0:128]
        onesblk = cst[:, 128:256]
        pswap = cst[:, 256:384]
        ident = cst[:, 384:512]
        rope = sb("rope_sb", [128, 2 * S], BF16)
        rope_t = Tile()
        prm = sb("prm_sb", [128, P_LAM], F32)
        prm_t = Tile()
        drv = sb("drv", [128, NDRV], F32)
        drv_t = Tile()
        WSLOT = 3072
        wreg = sb("wreg", [128, 3 * WSLOT], BF16)
        wslots = Ring([(wreg[:, i * WSLOT:(i + 1) * WSLOT], Tile()) for i in range(3)])
        wslots9 = Ring([(wreg[:, i * 1024:(i + 1) * 1024], Tile()) for i in range(9)])
        sqr = Ring([(sb(f"sq{i}", [128, 512], BF16), Tile()) for i in range(2)])
        f32r = Ring([(sb(f"f32t{i}", [128, 512], F32), Tile()) for i in range(4)])
        lnm, lnm_t = sb("lnm", [128, 512], F32), Tile()
        lnv, lnv_t = sb("lnv", [128, 512], F32), Tile()
        hsp_t = [Tile() for _ in range(NT)]
        bfr = Ring([(sb(f"bft{i}", [128, 512], BF16), Tile()) for i in range(5)])
        AR_WORDS = 96 * 256
        arena = sb("arena", [128, AR_WORDS], F32)

        def carve(off_kib, n_elems, dt, pat=None, **kw):
            off = int(round(off_kib * 256))
            if dt == F32:
                ap = arena[:, off:off + n_elems]
            else:
                assert n_elems % 2 == 0
                ap = arena[:, off:off + n_elems // 2].bitcast(BF16)
            if pat:
                ap = ap.rearrange(pat, **kw)
            return ap

        psb = [(stack.enter_context(nc.psum_tensor(f"ps{i}", [128, 512], F32)), Tile()) for i in range(8)]
        ps_all = Ring(psb)

        k.dma("pool", cst[:], cst_d[:, :], writes=[cst_t])
        k.dma("pool", rope[:], rope_d[:, :], writes=[rope_t])
        k.dma("sp", prm[:], prm_d[:, 0:P_LAM], writes=[prm_t])
        k.ts(drv[:, 0:32], prm[:, 0:32], 32.0, None, ALU.mult, None, [prm_t], [drv_t])
        k.ts(drv[:, V_QN:V_QN + 2], prm[:, P_QN:P_QN + 2], 8.0, None, ALU.mult, None, [prm_t], [drv_t])
        k.ts(drv[:, V_SUB:V_SUB + 1], prm[:, P_SUB:P_SUB + 1], math.sqrt(128.0) * (1.0 - LAM_INIT), None, ALU.mult, None,
             [prm_t], [drv_t])
        k.ts(drv[:, V_MQN:V_MQN + 4], prm[:, P_MQN:P_MQN + 4], 16.0, None, ALU.mult, None, [prm_t], [drv_t])
        lp_ap, lp_t = f32r.next()
        k.dma("sp", lp_ap[:, 0:256], prm_d[:, P_LAM:P_LAM + 256], writes=[lp_t])
        lt_ap, lt_t = f32r.next()
        k.tt(lt_ap[:, 0:64], lp_ap[:, 0:64], lp_ap[:, 64:128], ALU.mult, [lp_t], [lt_t])
        k.tt(lt_ap[:, 64:128], lp_ap[:, 128:192], lp_ap[:, 192:256], ALU.mult, [lp_t], [lt_t])
        k.op("dve", lambda e: e.reduce_sum(out=drv[:, V_TMP:V_TMP + 1], in_=lt_ap[:, 0:64], axis=AX.X), [lt_t], [drv_t])
        k.op("dve", lambda e: e.reduce_sum(out=drv[:, V_TMP + 1:V_TMP + 2], in_=lt_ap[:, 64:128], axis=AX.X), [lt_t], [drv_t])
        k.act(drv[:, V_TMP + 2:V_TMP + 4], drv[:, V_TMP:V_TMP + 2], AF.Exp, [drv_t], [drv_t])
        k.tt(drv[:, V_TMP + 4:V_TMP + 5], drv[:, V_TMP + 3:V_TMP + 4], drv[:, V_TMP + 2:V_TMP + 3], ALU.subtract, [drv_t], [drv_t])
        k.ts(drv[:, V_NLAM:V_NLAM + 1], drv[:, V_TMP + 4:V_TMP + 5], -LAM_INIT, None, ALU.add, None, [drv_t], [drv_t])
        CONSTS = [cst_t, rope_t, prm_t, drv_t]

        def wload(src_ap, n_elems):
            ap, t = wslots.next()
            k.dma("pool", ap[:, 0:n_elems], src_ap, writes=[t])
            return ap, t

        wc_t = {nm: [Tile() for _ in range(DC)] for nm in wc_d}

        def wload_c(nm, j, piece=0):
            ap, t = wslots9.next()
            k.dma("sp", ap[:, 0:1024], wc_d[nm][j][:, piece * 1024:(piece + 1) * 1024], reads=[wc_t[nm][j]], writes=[t])
            return ap, t

        for nm, (src, n) in wc_src.items():
            for j in range(DC):
                ap, t = wload(src[j], n)
                k.dma("sp", wc_d[nm][j], ap[:, 0:n], reads=[t], writes=[wc_t[nm][j]])
        k.barrier()

        def rms_tile(src_fn, src_tiles, nch, ncol, eps_n, ps_pick):
            ps, pst = ps_pick()
            for c in range(nch):
                sq, sqt = sqr.next()
                k.act(sq[:, 0:ncol], src_fn(c), AF.Square, [src_tiles[c]], [sqt])
                k.mm(ps[:, 0:ncol], pst, ones, sq[:, 0:ncol], c == 0, c == nch - 1, [sqt, cst_t], force=True)
            rs, rst = f32r.next()
            k.rstd(rs[:, 0:ncol], ps[:, 0:ncol], eps_n, [pst], rst)
            return rs, rst

        def norm_full(hT, hT_t, gcol, spill):
            for t in range(NT):
                sl = slice(t * TT, (t + 1) * TT)
                rs, rst = rms_tile(lambda c: xT[:, c, sl], [xT_t[c][t] for c in range(DC)], DC, TT, D * EPS, ps_all.next)
                for c in range(DC):
                    k.stt(hT[:, c, sl], xT[:, c, sl], drv[:, gcol + c:gcol + c + 1], rs[:, 0:TT], ALU.mult, ALU.mult,
                          [xT_t[c][t], rst, drv_t], [hT_t[c][t]])
                if spill:
                    k.dma("sp", hsp_v[:, :, sl], hT[:, :, sl], reads=[hT_t[c][t] for c in range(DC)], writes=[hsp_t[t]])

        def ffn(idx, gcol):
            hT = carve(0, DC * S, BF16, "p (c t) -> p c t", c=DC)
            hT_t = [[Tile() for _ in range(NT)] for _ in range(DC)]
            actT = carve(32 * S / 2048, FC * SH, BF16, "p (c t) -> p c t", c=FC)
            act_t = [[Tile() for _ in range(NTH)] for _ in range(FC)]
            norm_full(hT, hT_t, gcol, False)
            for half in range(2):
                for j in range(FC):
                    w, wt = wload(wgu_d[idx][j], 2 * DC * 128)
                    wv = w[:, 0:2 * DC * 128].rearrange("p (g c m) -> p g c m", g=2, c=DC)
                    pg = [ps_all.next() for _ in range(NTH)]
                    pu = [ps_all.next() for _ in range(NTH)]
                    for g, pp in ((0, pg), (1, pu)):
                        for c in range(DC):
                            for tl in range(NTH):
                                t = half * NTH + tl
                                k.mm(pp[tl][0][:, :], pp[tl][1], wv[:, g, c, :], hT[:, c, t * TT:(t + 1) * TT],
                                     c == 0, c == DC - 1, [wt, hT_t[c][t]])
                    for tl in range(NTH):
                        sg, sgt = bfr.next()
                        k.act(sg[:, :], pg[tl][0][:, :], AF.Silu, [pg[tl][1]], [sgt])
                        k.tt(actT[:, j, tl * TT:(tl + 1) * TT], pu[tl][0][:, :], sg[:, :], ALU.mult,
                             [pu[tl][1], sgt], [act_t[j][tl]])
                for c in range(DC):
                    w, wt = wload(wdn_d[idx][c], FC * 128)
                    wv = w[:, 0:FC * 128].rearrange("p (f m) -> p f m", f=FC)
                    pd = [ps_all.next() for _ in range(NTH)]
                    for f in range(FC):
                        for tl in range(NTH):
                            k.mm(pd[tl][0][:, :], pd[tl][1], wv[:, f, :], actT[:, f, tl * TT:(tl + 1) * TT],
                                 f == 0, f == FC - 1, [wt, act_t[f][tl]])
                    for tl in range(NTH):
                        t = half * NTH + tl
                        sl = slice(t * TT, (t + 1) * TT)
                        k.stt(xT[:, c, sl], pd[tl][0][:, :], 0.5, xT[:, c, sl], ALU.mult, ALU.add,
                              [pd[tl][1], xT_t[c][t]], [xT_t[c][t]])
            k.barrier()

        def gated_proj(w_nm, jc, y_rhs_fn, y_reads_fn, h_rhs_fn, h_reads_fn, bcol, ncol, emit):
            wg, wgt = wload_c(w_nm, jc, 1)
            wy, wyt = wload_c(w_nm, jc, 0)
            wgv = wg[:, 0:DC * 128].rearrange("p (c m) -> p c m", c=DC)
            wyv = wy[:, 0:DC * 128].rearrange("p (c m) -> p c m", c=DC)
            py, pyt = ps_all.next()
            pg, pgt = ps_all.next()
            for c in range(DC):
                k.mm(pg[:, 0:ncol], pgt, wgv[:, c, :], h_rhs_fn(c), c == 0, c == DC - 1, [wgt] + h_reads_fn(c))
            for c in range(DC):
                k.mm(py[:, 0:ncol], pyt, wyv[:, c, :], y_rhs_fn(c), c == 0, c == DC - 1, [wyt] + y_reads_fn(c))
            g, gt = f32r.next()
            k.act(g[:, 0:ncol], pg[:, 0:ncol], AF.Sigmoid, [pgt, prm_t], [gt], bias=prm[:, P_BG + bcol:P_BG + bcol + 1])
            emit(py, pyt, g, gt)

        def wo_apply(m_fn, m_reads_fn, t0, ncol, xtiles_fn):
            for oc in range(DC):
                w, wt = wload_c("wo", oc)
                wv = w[:, 0:DC * 128].rearrange("p (c m) -> p c m", c=DC)
                po, pot = ps_all.next()
                for c in range(DC):
                    k.mm(po[:, 0:ncol], pot, wv[:, c, :], m_fn(c), c == 0, c == DC - 1, [wt] + m_reads_fn(c))
                xt = xtiles_fn(oc)
                k.tt(xT[:, oc, t0:t0 + ncol], po[:, 0:ncol], xT[:, oc, t0:t0 + ncol], ALU.add, [pot, xt], [xt])

        def memkv(b, mkT, mk_t, mv, mv_t):
            memT = carve(32, DC * MEM, F32, "p (c t) -> p c t", c=DC)
            memT_t = [Tile() for _ in range(DC)]
            memn = carve(40, DC * MEM, BF16, "p (c t) -> p c t", c=DC)
            memn_t = [Tile() for _ in range(DC)]
            for c in range(DC):
                k.dma("sp", memT[:, c, :], memT_d[b][:, c * MEM:(c + 1) * MEM], writes=[memT_t[c]])
            rs, rst = rms_tile(lambda c: memT[:, c, :], memT_t, DC, MEM, D * EPS, ps_all.next)
            for c in range(DC):
                k.stt(memn[:, c, :], memT[:, c, :], drv[:, V_MEMN + c:V_MEMN + c + 1], rs[:, 0:MEM], ALU.mult, ALU.mult,
                      [memT_t[c], rst, drv_t], [memn_t[c]])
            for hm in range(4):
                pk = []
                for i in range(2):
                    fc = 2 * hm + i
                    w, wt = wload(wkvk_d[fc], DC * 128)
                    wv = w[:, 0:DC * 128].rearrange("p (c m) -> p c m", c=DC)
                    p, pt = ps_all.next()
                    for c in range(DC):
                        k.mm(p[:, 0:MEM], pt, wv[:, c, :], memn[:, c, :], c == 0, c == DC - 1, [wt, memn_t[c]])
                    pk.append((p, pt))
                rs, rst = rms_tile(lambda i: pk[i][0][:, 0:MEM], [pk[0][1], pk[1][1]], 2, MEM, 256 * EPS, ps_all.next)
                for i in range(2):
                    fc = 2 * hm + i
                    k.stt(mkT[:, fc, :], pk[i][0][:, 0:MEM], drv[:, V_MKN + i:V_MKN + i + 1], rs[:, 0:MEM], ALU.mult, ALU.mult,
                          [pk[i][1], rst, drv_t], [mk_t[fc]])
            for qd in range(4):
                w, wt = wload(wkvv_d[qd], DC * 256)
                wv = w[:, 0:DC * 256].rearrange("p (c n) -> p c n", c=DC)
                for tk in range(2):
                    p, pt = ps_all.next()
                    for c in range(DC):
                        k.mm(p[:, 0:256], pt, memn[:, c, tk * 128:(tk + 1) * 128], wv[:, c, :], c == 0, c == DC - 1,
                             [wt, memn_t[c]])
                    k.act(mv[:, tk, qd * 256:(qd + 1) * 256], p[:, 0:256], AF.Copy, [pt], [mv_t[tk]])

        def qk_partA(px, pxt, gcol, ncol):
            y, yt = bfr.next()
            k.act(y[:, 0:ncol], px[:, 0:ncol], AF.Identity, [pxt, drv_t], [yt], scale=drv[:, gcol:gcol + 1])
            sq, sqt = sqr.next()
            k.act(sq[:, 0:ncol], px[:, 0:ncol], AF.Square, [pxt], [sqt])
            return (y, yt, sq, sqt)

        def qk_partB(stA, out_ap, out_t, t0, ncol):
            y, yt, sq, sqt = stA
            pss, psst = ps_all.next()
            k.mm(pss[:, 0:ncol], psst, onesblk, sq[:, 0:ncol], True, True, [sqt, cst_t])
            psp, pspt = ps_all.next()
            k.mm(psp[:, 0:ncol], pspt, pswap, y[:, 0:ncol], True, True, [yt, cst_t])
            rs, rst = f32r.next()
            k.rstd(rs[:, 0:ncol], pss[:, 0:ncol], 64 * EPS, [psst], rst)
            t1, t1t = f32r.next()
            k.tt(t1[:, 0:ncol], y[:, 0:ncol], rope[:, t0:t0 + ncol], ALU.mult, [yt, rope_t], [t1t])
            t2, t2t = f32r.next()
            k.tt(t2[:, 0:ncol], psp[:, 0:ncol], rope[:, S + t0:S + t0 + ncol], ALU.mult, [pspt, rope_t], [t2t])
            k.tt(t1[:, 0:ncol], t1[:, 0:ncol], t2[:, 0:ncol], ALU.add, [t1t, t2t], [t1t])
            if isinstance(out_ap, list):
                for prow, dst in out_ap:
                    k.tt(dst, t1[prow, 0:ncol], rs[prow, 0:ncol], ALU.mult, [t1t, rst], [out_t])
            else:
                k.tt(out_ap, t1[:, 0:ncol], rs[:, 0:ncol], ALU.mult, [t1t, rst], [out_t])

        def mixer(b):
            sc = S / 2048.0
            mkT = carve(88, DC * MEM, BF16, "p (c t) -> p c t", c=DC)
            mk_t = [Tile() for _ in range(DC)]
            mv = carve(92, 2 * D, BF16, "p (k n) -> p k n", k=2)
            mv_t = [Tile() for _ in range(2)]
            if "att" in stages:
                memkv(b, mkT, mk_t, mv, mv_t)
                k.barrier()
            hT = carve(0, DC * S, BF16, "p (c t) -> p c t", c=DC)
            hT_t = [[Tile() for _ in range(NT)] for _ in range(DC)]
            norm_full(hT, hT_t, V_MIX, True)

            if "conv" in stages:
                cT = carve(32 * sc, DC * S, BF16, "p (c t) -> p c t", c=DC)
                cT_t = [[Tile() for _ in range(NT)] for _ in range(DC)]
                mT = carve(64 * sc, DC * TT, BF16, "p (c t) -> p c t", c=DC)
                mT_t = [Tile() for _ in range(DC)]
                upad = carve(64 * sc + 8, S + 2 * PAD + 2, BF16)
                upad_t = Tile()
                dg = carve(64 * sc + 8 + (S + 32) * 2 / 1024.0 + 0.05, CW * 128, BF16, "p (j m) -> p j m", j=CW)
                dg_t = Tile()
                k.op("dve", lambda e: e.memset(upad[:, 0:S + 2 * PAD + 2], 0.0), [], [upad_t])
                for c in range(DC):
                    w, wt = wload(wab_d[c], 2 * DC * 128)
                    wv = w[:, 0:2 * DC * 128].rearrange("p (g c m) -> p g c m", g=2, c=DC)
                    for j in range(CW):
                        k.ts(dg[:, j, :], ident, prm[:, P_CW + c * CW + j:P_CW + c * CW + j + 1], None, ALU.mult, None,
                             [cst_t, prm_t], [dg_t])
                    for t in range(NT):
                        sl = slice(t * TT, (t + 1) * TT)
                        pa, pat = ps_all.next()
                        pb, pbt = ps_all.next()
                        for g, (pp, ppt) in ((0, (pa, pat)), (1, (pb, pbt))):
                            for cc in range(DC):
                                k.mm(pp[:, :], ppt, wv[:, g, cc, :], hT[:, cc, sl], cc == 0, cc == DC - 1, [wt, hT_t[cc][t]])
                        sg, sgt = bfr.next()
                        k.act(sg[:, :], pb[:, :], AF.Sigmoid, [pbt], [sgt])
                        k.tt(upad[:, PAD + t * TT:PAD + (t + 1) * TT], pa[:, :], sg[:, :], ALU.mult, [pat, sgt], [upad_t])
                    for t in range(NT):
                        pc, pct = ps_all.next()
                        for j in range(CW):
                            k.mm(pc[:, :], pct, dg[:, j, :], upad[:, t * TT + j:t * TT + j + TT], j == 0, j == CW - 1,
                                 [dg_t, upad_t])
                        k.act(cT[:, c, t * TT:(t + 1) * TT], pc[:, :], AF.Identity, [pct, prm_t], [cT_t[c][t]],
                              bias=prm[:, P_CB + c:P_CB + c + 1])
                k.barrier()
                def conv_ln(t):
                    sl = slice(t * TT, (t + 1) * TT)
                    pm, pmt = ps_all.next()
                    pq, pqt = ps_all.next()
                    for c in range(DC):
                        sq, sqt = sqr.next()
                        k.act(sq[:, :], cT[:, c, sl], AF.Square, [cT_t[c][t]], [sqt])
                        k.mm(pm[:, :], pmt, ones, cT[:, c, sl], c == 0, c == DC - 1, [cT_t[c][t], cst_t])
                        k.mm(pq[:, :], pqt, ones, sq[:, :], c == 0, c == DC - 1, [sqt, cst_t], force=True)
                    mean, meant = lnm, lnm_t
                    k.ts(mean[:, :], pm[:, :], 1.0 / D, None, ALU.mult, None, [pmt], [meant])
                    var, vart = lnv, lnv_t
                    k.tt(var[:, :], mean[:, :], mean[:, :], ALU.mult, [meant], [vart])
                    k.stt(var[:, :], pq[:, :], 1.0 / D, var[:, :], ALU.mult, ALU.subtract, [pqt, vart], [vart])
                    k.rstd(var[:, :], var[:, :], EPS, [vart], vart)
                    k.stt(mean[:, :], mean[:, :], -1.0, var[:, :], ALU.mult, ALU.mult, [meant, vart], [meant])
                    for c in range(DC):
                        tmp, tmpt = f32r.next()
                        k.tt(tmp[:, :], cT[:, c, sl], var[:, :], ALU.mult, [cT_t[c][t], vart], [tmpt])
                        k.tt(tmp[:, :], tmp[:, :], mean[:, :], ALU.add, [tmpt, meant], [tmpt])
                        k.act(cT[:, c, sl], tmp[:, :], AF.Silu, [tmpt, prm_t], [cT_t[c][t]],
                              bias=prm[:, P_LNB + c:P_LNB + c + 1], scale=prm[:, P_LNG + c:P_LNG + c + 1])

                def conv_proj(t):
                    sl = slice(t * TT, (t + 1) * TT)
                    for jc in range(DC):
                        def emit(py, pyt, g, gt, jc=jc):
                            k.tt(mT[:, jc, :], py[:, :], g[:, :], ALU.mult, [pyt, gt], [mT_t[jc]])
                        gated_proj("wcg", jc, lambda c: cT[:, c, sl], lambda c: [cT_t[c][t]],
                                   lambda c: hT[:, c, sl], lambda c: [hT_t[c][t]], jc, TT, emit)
                    wo_apply(lambda c: mT[:, c, :], lambda c: [mT_t[c]], t * TT, TT, lambda oc: xT_t[oc][t])

                conv_ln(0)
                for t in range(NT):
                    if t + 1 < NT:
                        conv_ln(t + 1)
                    conv_proj(t)
                k.barrier()

            if "att" in stages:
                k.barrier()
                kT = carve(0, DC * S, BF16, "p (c t) -> p c t", c=DC)
                kT_t = [[Tile() for _ in range(NT)] for _ in range(DC)]
                V = carve(32 * sc, NK * D, BF16, "p (k n) -> p k n", k=NK)
                V_t = [Tile() for _ in range(NK)]
                hbuf = [(carve(64 * sc + 8 * i, DC * TT, BF16, "p (c t) -> p c t", c=DC), Tile()) for i in range(2)]
                for t in range(NT):
                    hb, hbt = hbuf[t % 2]
                    k.dma("sp", hb[:, :, :], hsp_v[:, :, t * TT:(t + 1) * TT], reads=[hsp_t[t]], writes=[hbt])
                    prevB = None
                    for kc in range(DC):
                        w, wt = wload(wk_d[kc], DC * 128)
                        wv = w[:, 0:DC * 128].rearrange("p (c m) -> p c m", c=DC)
                        px, pxt = ps_all.next()
                        for c in range(DC):
                            k.mm(px[:, :], pxt, wv[:, c, :], hb[:, c, :], c == 0, c == DC - 1, [wt, hbt])
                        stA = qk_partA(px, pxt, V_KN, TT)
                        if prevB is not None:
                            qk_partB(*prevB)
                        prevB = (stA, kT[:, kc, t * TT:(t + 1) * TT], kT_t[kc][t], t * TT, TT)
                    qk_partB(*prevB)
                    for qd in range(4):
                        w, wt = wload(wv_d[qd], DC * 256)
                        wv = w[:, 0:DC * 256].rearrange("p (c n) -> p c n", c=DC)
                        for tk in range(4):
                            kk = t * 4 + tk
                            p, pt = ps_all.next()
                            for c in range(DC):
                                k.mm(p[:, 0:256], pt, hb[:, c, tk * 128:(tk + 1) * 128], wv[:, c, :], c == 0, c == DC - 1,
                                     [wt, hbt])
                            if (qd + tk) % 2 == 0:
                                k.act(V[:, kk, qd * 256:(qd + 1) * 256], p[:, 0:256], AF.Copy, [pt], [V_t[kk]])
                            else:
                                k.op("dve", lambda e, kk=kk, qd=qd, p=p: e.tensor_copy(out=V[:, kk, qd * 256:(qd + 1) * 256],
                                                                                        in_=p[:, 0:256]), [pt], [V_t[kk]])
                k.barrier()
                hq = [(carve(64 * sc + 4 * i, DC * QT, BF16, "p (c t) -> p c t", c=DC), Tile()) for i in range(2)]
                Qb = carve(64 * sc + 8, DC * 2 * QT, BF16, "p (c t) -> p c t", c=DC)
                Qb_t = [Tile() for _ in range(DC)]
                oT = carve(64 * sc + 16, DC * QT, BF16, "p (c t) -> p c t", c=DC)
                oT_t = [Tile() for _ in range(DC)]
                mg = carve(64 * sc + 20, DC * QT, BF16, "p (c t) -> p c t", c=DC)
                mg_t = [Tile() for _ in range(DC)]
                mqT, mq_t = mg, mg_t
                acc, acc_t = lnm, lnm_t
                ps_lo = Ring(psb[0:4])
                ps_hi = Ring(psb[6:8])
                for c in range(DC):
                    k.op("dve", lambda e, c=c: e.memset(Qb[:, c, :], 0.0), [], [Qb_t[c]])
                def hq_load(qj):
                    hbj, hbtj = hq[qj % 2]
                    k.dma("sp", hbj[:, :, :], hsp_v[:, :, qj * QT:(qj + 1) * QT], reads=[hsp_t[qj * QT // TT]], writes=[hbtj])

                for qi in range(NQ):
                    q0 = qi * QT
                    tq = q0 // TT
                    hb, hbt = hq[qi % 2]
                    if qi == 0:
                        hq_load(0)
                    mst = [dict() for _ in range(4)]

                    def m1a(hm):
                        pk = []
                        for i in range(2):
                            fc = 2 * hm + i
                            w, wt = wload_c("wmq", fc)
                            wv = w[:, 0:DC * 128].rearrange("p (c m) -> p c m", c=DC)
                            p, pt = ps_all.next()
                            for c in range(DC):
                                k.mm(p[:, 0:QT], pt, wv[:, c, :], hb[:, c, :], c == 0, c == DC - 1, [wt, hbt])
                            pk.append((p, pt))
                        mst[hm]["pk"] = pk

                    def m1b(hm):
                        pk = mst[hm]["pk"]
                        rs, rst = rms_tile(lambda i: pk[i][0][:, 0:QT], [pk[0][1], pk[1][1]], 2, QT, 256 * EPS, ps_all.next)
                        for i in range(2):
                            fc = 2 * hm + i
                            k.stt(mqT[:, fc, :], pk[i][0][:, 0:QT], drv[:, V_MQN + i:V_MQN + i + 1], rs[:, 0:QT],
                                  ALU.mult, ALU.mult, [pk[i][1], rst, drv_t], [mq_t[fc]])

                    def m2(hm):
                        psc, psct = ps_all.next()
                        for mc in range(2):
                            for i in range(2):
                                fc = 2 * hm + i
                                k.mm(psc[:, mc * QT:(mc + 1) * QT], psct, mkT[:, fc, mc * 128:(mc + 1) * 128], mqT[:, fc, :],
                                     i == 0, i == 1, [mk_t[fc], mq_t[fc]])
                        pm_, pmt_ = bfr.next()
                        k.act(pm_[:, :], psc[:, :], AF.Exp, [psct], [pmt_], scale=1.0 / 16.0)
                        mst[hm]["pm"] = (pm_, pmt_)

                    def m3(hm):
                        pm_, pmt_ = mst[hm]["pm"]
                        pso = [ps_all.next() for _ in range(2)]
                        pss, psst = ps_all.next()
                        for e2 in range(2):
                            for mc in range(2):
                                k.mm(pso[e2][0][:, 0:QT], pso[e2][1], mv[:, mc, hm * 256 + e2 * 128:hm * 256 + (e2 + 1) * 128],
                                     pm_[:, mc * QT:(mc + 1) * QT], mc == 0, mc == 1, [mv_t[mc], pmt_])
                        for mc in range(2):
                            k.mm(pss[:, 0:QT], psst, ones, pm_[:, mc * QT:(mc + 1) * QT], mc == 0, mc == 1, [pmt_, cst_t])
                        rr, rrt = f32r.next()
                        k.recip(rr[:, 0:QT], pss[:, 0:QT], [psst], rrt)
                        for e2 in range(2):
                            fc = 2 * hm + e2
                            k.tt(oT[:, fc, :], pso[e2][0][:, 0:QT], rr[:, 0:QT], ALU.mult, [pso[e2][1], rrt], [oT_t[fc]])

                    for fn, hm in ((m1a, 0), (m1a, 1), (m1b, 0), (m1a, 2), (m1b, 1), (m2, 0), (m1a, 3), (m1b, 2), (m2, 1),
                                   (m3, 0), (m1b, 3), (m2, 2), (m3, 1), (m2, 3), (m3, 2), (m3, 3)):
                        fn(hm)
                    for jc in range(DC):
                        def emit(py, pyt, g, gt, jc=jc):
                            k.tt(mg[:, jc, :], py[:, 0:QT], g[:, 0:QT], ALU.mult, [pyt, gt], [mg_t[jc]])
                        gated_proj("wmg", jc, lambda c: oT[:, c, :], lambda c: [oT_t[c]],
                                   lambda c: hb[:, c, :], lambda c: [hbt], 16 + jc, QT, emit)
                    prevB = None
                    for qc in range(DC):
                        w, wt = wload_c("wq", qc)
                        wv = w[:, 0:DC * 128].rearrange("p (c m) -> p c m", c=DC)
                        px, pxt = ps_all.next()
                        for c in range(DC):
                            k.mm(px[:, 0:QT], pxt, wv[:, c, :], hb[:, c, :], c == 0, c == DC - 1, [wt, hbt])
                        stA = qk_partA(px, pxt, V_QN, QT)
                        if prevB is not None:
                            qk_partB(*prevB)
                        prevB = (stA, [(slice(0, 64), Qb[0:64, qc, 0:QT]), (slice(64, 128), Qb[64:128, qc, QT:2 * QT])],
                                 Qb_t[qc], q0, QT)
                    qk_partB(*prevB)
                    def epilogue_stages(h, po, pot, acc, acc_t):
                        st = {}

                        def s1():
                            st["accb"] = bfr.next()
                            k.act(st["accb"][0][:, :], acc[:, :], AF.Copy, [acc_t], [st["accb"][1]])

                        def s1b():
                            st["psm"] = ps_hi.next()
                            k.mm(st["psm"][0][:, :], st["psm"][1], ones, st["accb"][0][:, :], True, True, [st["accb"][1], cst_t])

                        def s2():
                            st["rr"] = f32r.items[0]
                            k.recip(st["rr"][0][:, :], st["psm"][0][:, :], [st["psm"][1]], st["rr"][1])

                        def s3():
                            rr, rrt = st["rr"]
                            k.tt(rr[:, :], po[:, :], rr[:, :], ALU.mult, [pot, rrt], [rrt])
                            st["r0"] = f32r.items[1]
                            r0, r0t = st["r0"]
                            k.stt(r0[:, 0:QT], rr[:, QT:2 * QT], drv[:, V_NLAM:V_NLAM + 1], rr[:, 0:QT], ALU.mult, ALU.add,
                                  [rrt, drv_t], [r0t])

                        def s4():
                            r0, r0t = st["r0"]
                            st["sq"] = sqr.next()
                            k.act(st["sq"][0][:, 0:QT], r0[:, 0:QT], AF.Square, [r0t], [st["sq"][1]])

                        def s4b():
                            sq, sqt = st["sq"]
                            st["pn"] = ps_hi.next()
                            k.mm(st["pn"][0][:, 0:QT], st["pn"][1], ones, sq[:, 0:QT], True, True, [sqt, cst_t])

                        def s5():
                            r0, r0t = st["r0"]
                            k.rstd(r0[:, QT:2 * QT], st["pn"][0][:, 0:QT], 128 * EPS, [st["pn"][1]], r0t)

                        def s6():
                            r0, r0t = st["r0"]
                            k.stt(oT[:, h, :], r0[:, 0:QT], drv[:, V_SUB:V_SUB + 1], r0[:, QT:2 * QT], ALU.mult, ALU.mult,
                                  [r0t, drv_t], [oT_t[h]])

                        return [s1, s1b, s2, s3, s4, s4b, s5, s6]

                    if qi + 1 < NQ:
                        hq_load(qi + 1)
                    pending_epi = []
                    for h in range(DC):
                        po, pot = psb[4 + (h % 2)]
                        acc, acc_t = (lnm, lnm_t) if h % 2 == 0 else (lnv, lnv_t)
                        pend = []

                        def pv_step(kc, pe0, pet0):
                            k.mm(po[:, :], pot, V[:, kc, h * 128:(h + 1) * 128], pe0[:, :], kc == 0, kc == NK - 1, [V_t[kc], pet0])
                            if kc == 0:
                                k.op("dve", lambda e: e.tensor_copy(out=acc[:, :], in_=pe0[:, :]), [pet0], [acc_t])
                            else:
                                k.tt(acc[:, :], acc[:, :], pe0[:, :], ALU.add, [acc_t, pet0], [acc_t])

                        for kc in range(NK):
                            psc, psct = ps_lo.next()
                            k.mm(psc[:, :], psct, kT[:, h, kc * 128:(kc + 1) * 128], Qb[:, h, :], True, True,
                                 [kT_t[h][kc // 4], Qb_t[h]])
                            pe_, pet_ = bfr.next()
                            k.act(pe_[:, :], psc[:, :], AF.Exp, [psct], [pet_], scale=0.125)
                            pend.append((kc, pe_, pet_))
                            if len(pend) > 1:
                                pv_step(*pend.pop(0))
                            if pending_epi and kc >= 1 and (kc % 2 == 1 or NK < 16):
                                pending_epi.pop(0)()
                        pv_step(*pend.pop(0))
                        while pending_epi:
                            pending_epi.pop(0)()
                        pending_epi = epilogue_stages(h, po, pot, acc, acc_t)
                    while pending_epi:
                        pending_epi.pop(0)()
                    for jc in range(DC):
                        def emit(py, pyt, g, gt, jc=jc):
                            tmp, tmpt = f32r.next()
                            k.tt(tmp[:, 0:QT], py[:, 0:QT], g[:, 0:QT], ALU.mult, [pyt, gt], [tmpt])
                            k.tt(mg[:, jc, :], tmp[:, 0:QT], mg[:, jc, :], ALU.add, [tmpt, mg_t[jc]], [mg_t[jc]])
                        gated_proj("wdg", jc, lambda c: oT[:, c, :], lambda c: [oT_t[c]],
                                   lambda c: hb[:, c, :], lambda c: [hbt], 8 + jc, QT, emit)
                    wo_apply(lambda c: mg[:, c, :], lambda c: [mg_t[c]], q0, QT, lambda oc: xT_t[oc][tq])
                k.barrier()

        for b in range(NB):
            for c in range(DC):
                for t in range(NT):
                    k.dma("sp", xT[:, c, t * TT:(t + 1) * TT], xT_d[b][:, c * S + t * TT:c * S + (t + 1) * TT],
                          writes=[xT_t[c][t]])
            if "ffn1" in stages:
                ffn(0, V_FFN1)
            if "conv" in stages or "att" in stages:
                mixer(b)
            if "ffn2" in stages:
                ffn(1, V_FFN2)
            for c in range(DC):
                for t in range(NT):
                    k.dma("sp", out_d[b][:, c * S + t * TT:c * S + (t + 1) * TT], xT[:, c, t * TT:(t + 1) * TT],
                          reads=[xT_t[c][t]])
        k.barrier()
        build_program.last_nins = k.nins
    return nc


def _colblock(W, o, n=128):
    K = W.shape[0]
    return np.ascontiguousarray(W[:, o:o + n].reshape(K // 128, 128, n).transpose(1, 0, 2))


def _vec(v):
    return np.ascontiguousarray(v.reshape(-1, 128).T)


def rope_tables(S):
    half = 32
    inv_freq = (1.0 / (np.float32(10000.0) ** (np.arange(0, 64, 2, dtype=np.float32) / np.float32(64)))).astype(np.float32)
    ang = (np.arange(S, dtype=np.float32)[:, None] * inv_freq[None, :]).astype(np.float32)
    cos = np.cos(ang).astype(np.float32)
    sin = np.sin(ang).astype(np.float32)
    p = np.arange(128)
    C = cos[:, p % half].T
    sign = np.where((p % 64) < half, -1.0, 1.0).astype(np.float32)[:, None]
    Ss = sin[:, p % half].T * sign
    return np.ascontiguousarray(np.concatenate([C, Ss], axis=1).astype(np.float32))


def const_tables():
    p = np.arange(128)
    ones = np.ones((128, 128), np.float32)
    onesblk = (p[:, None] // 64 == p[None, :] // 64).astype(np.float32)
    sw = np.where((p % 64) < 32, p + 32, p - 32)
    pswap = (p[:, None] == sw[None, :]).astype(np.float32)
    ident = np.eye(128, dtype=np.float32)
    return np.ascontiguousarray(np.concatenate([ones, onesblk, pswap, ident], axis=1))


def prep_shared(inp, S):
    f = lambda a: np.asarray(a, dtype=np.float32)
    w_in = f(inp["w_in"])[0]
    c1, c2, c3, c4, c5 = 2048, 3072, 4096, 5120, 6144
    sh = {}
    prm = np.zeros((128, NPRM), np.float32)
    prm[:, 0:8] = _vec(f(inp["ffn1_norm"])[0])
    prm[:, 8:16] = _vec(f(inp["mix_norm"])[0])
    prm[:, 16:24] = _vec(f(inp["ffn2_norm"])[0])
    prm[:, 24:32] = _vec(f(inp["mem_norm"])[0])
    prm[:, P_BG:P_BG + 24] = _vec(f(inp["b_gate"])[0])
    prm[:, P_CB:P_CB + 8] = _vec(f(inp["conv_b"])[0])
    prm[:, P_LNG:P_LNG + 8] = _vec(f(inp["conv_ln_g"])[0])
    prm[:, P_LNB:P_LNB + 8] = _vec(f(inp["conv_ln_b"])[0])
    prm[:, P_QN] = np.tile(f(inp["diff_q_norm"])[0], 2)
    prm[:, P_KN] = np.tile(f(inp["diff_k_norm"])[0], 2)
    prm[:, P_SUB] = f(inp["diff_subln"])[0]
    prm[:, P_MQN:P_MQN + 2] = _vec(f(inp["mem_q_norm"])[0])
    prm[:, P_MKN:P_MKN + 2] = _vec(f(inp["mem_k_norm"])[0])
    cw = f(inp["conv_w"])[0][:, 0, :]
    prm[:, P_CW:P_CW + 248] = cw.T.reshape(DC, 128, CW).transpose(1, 0, 2).reshape(128, DC * CW)
    prm[:, P_LAM:P_LAM + 256] = np.broadcast_to(f(inp["diff_lambda"])[0].reshape(1, 256), (128, 256))
    sh["prm"] = prm
    sh["cst"] = const_tables()
    sh["rope"] = rope_tables(S)

    def blocks(W, offs, n=128):
        return np.ascontiguousarray(np.stack([_colblock(W, o, n).reshape(128, -1) for o in offs]))

    def blocks2(Wa, oa, Wb, ob):
        out = []
        for a, b_ in zip(oa, ob):
            out.append(np.concatenate([_colblock(Wa, a).reshape(128, -1), _colblock(Wb, b_).reshape(128, -1)], axis=1))
        return np.ascontiguousarray(np.stack(out))

    r8 = [i * 128 for i in range(8)]
    for i, nm in ((1, "ffn1"), (2, "ffn2")):
        wgu = f(inp[f"{nm}_w_gu"])[0]
        wdn = f(inp[f"{nm}_w_down"])[0]
        sh[f"wgu{i}"] = blocks2(wgu, [j * 128 for j in range(FC)], wgu, [FF + j * 128 for j in range(FC)])
        sh[f"wdn{i}"] = blocks(wdn, r8)
    sh["wab"] = blocks2(w_in, r8, w_in, [1024 + o for o in r8])
    sh["wq"] = blocks(w_in, [c1 + o for o in r8])
    sh["wk"] = blocks(w_in, [c2 + o for o in r8])
    sh["wv"] = blocks(w_in, [c3 + i * 256 for i in range(4)], 256)
    sh["wmq"] = blocks(w_in, [c4 + o for o in r8])
    wkv = f(inp["w_mem_kv"])[0]
    sh["wkvk"] = blocks(wkv, r8)
    sh["wkvv"] = blocks(wkv, [1024 + i * 256 for i in range(4)], 256)
    sh["wcg"] = blocks2(f(inp["w_conv_out"])[0], r8, w_in, [c5 + o for o in r8])
    sh["wdg"] = blocks2(f(inp["w_diff_out"])[0], r8, w_in, [c5 + 1024 + o for o in r8])
    sh["wmg"] = blocks2(f(inp["w_mem_out"])[0], r8, w_in, [c5 + 2048 + o for o in r8])
    sh["wo"] = blocks(f(inp["w_o"])[0], r8)
    return sh


def to_fm(x):
    nb, T, _ = x.shape
    return np.ascontiguousarray(x.reshape(nb, T, DC, 128).transpose(0, 3, 2, 1).reshape(nb, 128, DC * T))


def from_fm(y, T):
    nb = y.shape[0]
    return np.ascontiguousarray(y.reshape(nb, 128, DC, T).transpose(0, 3, 2, 1).reshape(nb, T, D))


_PROG_CACHE = {}


def run(inputs, S, NB, ncores, stages=("ffn1", "conv", "att", "ffn2")):
    x = np.asarray(inputs["x"], dtype=np.float32)
    mem = np.asarray(inputs["mem"], dtype=np.float32)
    sh = prep_shared(inputs, S)
    key = (S, NB, tuple(stages))
    if key not in _PROG_CACHE:
        _PROG_CACHE[key] = build_program(S, NB, stages)
    nc = _PROG_CACHE[key]
    in_maps = []
    for i in range(ncores):
        m = dict(sh)
        m["xT"] = to_fm(x[i * NB:(i + 1) * NB])
        m["memT"] = to_fm(mem[i * NB:(i + 1) * NB])
        in_maps.append(m)
    res = run_bass_kernel_spmd(nc, in_maps, core_ids=list(range(ncores)))
    outs = [from_fm(np.asarray(r["outT"]), S) for r in res.results]
    return np.concatenate(outs, axis=0)


def kernel(**inputs):
    B, S, _ = inputs["x"].shape
    NB = B // NCORES
    return run(inputs, S, NB, NCORES).astype(np.float32)
```

```python
import math
from contextlib import ExitStack

import numpy as np

import concourse.bass as bass
import concourse.mybir as mybir
from concourse.bass_utils import run_bass_kernel_spmd

F32 = mybir.dt.float32
BF16 = mybir.dt.bfloat16
AF = mybir.ActivationFunctionType
ALU = mybir.AluOpType
AX = mybir.AxisListType

D = 1024
DC = 8
FF = 2816
FC = 22
MEM = 256
CW = 31
PAD = 15
EPS = 1e-6
LAM_INIT = 0.2
NCORES = 8

P_NORMS = 0
P_BG = 32
P_CB = 56
P_LNG = 64
P_LNB = 72
P_QN = 80
P_KN = 81
P_SUB = 82
P_MQN = 83
P_MKN = 85
P_CW = 88
P_LAM = 336
NPRM = 592
V_FFN1 = 0
V_MIX = 8
V_FFN2 = 16
V_MEMN = 24
V_QN = 32
V_KN = 33
V_SUB = 34
V_MQN = 35
V_MKN = 37
V_NLAM = 39
V_TMP = 40
NDRV = 48

NDMA = 24
GEN = 16000


class Tile:
    __slots__ = ("w", "r")

    def __init__(self):
        self.w = None
        self.r = {}


class KB:
    def __init__(self, nc, stack):
        self.nc = nc
        self.stack = stack
        self.eng = {"pe": nc.tensor, "act": nc.scalar, "dve": nc.vector, "pool": nc.gpsimd, "sp": nc.sync}
        self.cnt = {e: 0 for e in self.eng}
        self.sems = {}
        self.seen = {e: {} for e in self.eng}
        self.dma_cnt = [0] * NDMA
        self.dma_rr = 0
        self.dma_rr_sw = 0
        self.nins = 0

    def sem(self, key):
        s = self.sems.get(key)
        if s is None:
            s = self.stack.enter_context(self.nc.semaphore(f"s_{key[0]}_{key[1]}"))
            self.sems[key] = s
        return s

    @staticmethod
    def sigkey(eng, count):
        gen = (count - 1) // GEN
        return (eng, gen), count - gen * GEN

    def _collect(self, eng, reads, writes, is_dma):
        need = {}

        def add(kv, raw):
            key, val, peng = kv
            if not is_dma and peng == eng and eng == "pe":
                return
            if need.get(key, 0) < val:
                need[key] = val

        for t in reads:
            if t.w is not None:
                add(t.w, True)
        for t in writes:
            if t.w is not None:
                add(t.w, False)
            for kv in t.r.values():
                add(kv, False)
        return need

    def _waits(self, eng, need):
        seen = self.seen[eng]
        e = self.eng[eng]
        for key, val in need.items():
            if seen.get(key, 0) < val:
                e.wait_ge(self.sem(key), val)
                seen[key] = val
                self.nins += 1

    def _update(self, kv, reads, writes):
        key = kv[0]
        for t in reads:
            old = t.r.get(key)
            if old is None or old[1] < kv[1]:
                t.r[key] = kv
        for t in writes:
            t.w = kv
            t.r = {}

    def op(self, eng, fn, reads=(), writes=(), signal=True):
        need = self._collect(eng, reads, writes, False)
        self._waits(eng, need)
        ins = fn(self.eng[eng])
        self.nins += 1
        if signal:
            self.cnt[eng] += 1
            key, val = self.sigkey(eng, self.cnt[eng])
            ins.then_inc(self.sem(key), 1)
        else:
            key, val = self.sigkey(eng, self.cnt[eng] + 1)
        self._update((key, val, eng), reads, writes)
        return ins

    def dma(self, q, out, in_, reads=(), writes=()):
        half = NDMA // 2
        if q == "pool":
            i = self.dma_rr_sw
            self.dma_rr_sw = (i + 1) % half
        else:
            i = half + self.dma_rr
            self.dma_rr = (self.dma_rr + 1) % half
        need = self._collect(q, reads, writes, True)
        key = ("dma", i)
        if self.dma_cnt[i] > 0:
            need[key] = max(need.get(key, 0), self.dma_cnt[i])
        self._waits(q, need)
        ins = self.eng[q].dma_start(out=out, in_=in_)
        self.nins += 1
        self.dma_cnt[i] += 16
        ins.then_inc(self.sem(key), 16)
        self._update((key, self.dma_cnt[i], "dma"), reads, writes)
        return ins

    def barrier(self, engines=("pe", "act", "dve", "pool", "sp")):
        need = {}
        for e in ("pe", "act", "dve", "pool"):
            if self.cnt[e] > 0:
                k, v = self.sigkey(e, self.cnt[e])
                need[k] = v
        for i in range(NDMA):
            if self.dma_cnt[i] > 0:
                need[("dma", i)] = self.dma_cnt[i]
        for e in engines:
            self._waits(e, dict(need))

    def mm(self, ps, pst, lhsT, rhs, start, stop, reads, force=False):
        return self.op("pe", lambda e: e.matmul(ps, lhsT, rhs, start=start, stop=stop),
                       reads=reads, writes=[pst], signal=(stop or force))

    def act(self, out, in_, func, reads, writes, bias=None, scale=None):
        kw = {}
        if bias is not None:
            kw["bias"] = bias
        if scale is not None:
            kw["scale"] = scale
        return self.op("act", lambda e: e.activation(out=out, in_=in_, func=func, **kw), reads=reads, writes=writes)

    def tt(self, out, in0, in1, op, reads, writes):
        return self.op("dve", lambda e: e.tensor_tensor(out=out, in0=in0, in1=in1, op=op), reads=reads, writes=writes)

    def ts(self, out, in0, s1, s2, op0, op1, reads, writes):
        if op1 is None:
            return self.op("dve", lambda e: e.tensor_scalar(out=out, in0=in0, scalar1=s1, scalar2=None, op0=op0),
                           reads=reads, writes=writes)
        return self.op("dve", lambda e: e.tensor_scalar(out=out, in0=in0, scalar1=s1, scalar2=s2, op0=op0, op1=op1),
                       reads=reads, writes=writes)

    def rstd(self, out, in_, eps_n, reads, wt):
        self.act(out, in_, AF.Ln, reads, [wt], bias=float(eps_n))
        self.act(out, out, AF.Exp, [wt], [wt], scale=-0.5)

    def recip(self, out, in_, reads, wt):
        self.act(out, in_, AF.Ln, reads, [wt])
        self.act(out, out, AF.Exp, [wt], [wt], scale=-1.0)

    def stt(self, out, in0, scalar, in1, op0, op1, reads, writes):
        return self.op("dve", lambda e: e.scalar_tensor_tensor(out=out, in0=in0, scalar=scalar, in1=in1, op0=op0, op1=op1),
                       reads=reads, writes=writes)


class Ring:
    def __init__(self, items):
        self.items = items
        self.i = 0

    def next(self):
        it = self.items[self.i]
        self.i = (self.i + 1) % len(self.items)
        return it


def build_program(S, NB, stages=("ffn1", "conv", "att", "ffn2")):
    TT = 512
    NT = S // TT
    QT = 256
    NQ = S // QT
    NK = S // 128
    assert S % 1024 == 0
    SH = S // 2
    NTH = NT // 2

    nc = bass.Bass("TRN2", target_bir_lowering=False)
    dr = {}

    def din(name, shape):
        dr[name] = nc.dram_tensor(name, list(shape), F32, kind="ExternalInput").ap()
        return dr[name]

    xT_d = din("xT", [NB, 128, DC * S])
    memT_d = din("memT", [NB, 128, DC * MEM])
    prm_d = din("prm", [128, NPRM])
    cst_d = din("cst", [128, 512])
    rope_d = din("rope", [128, 2 * S])
    wgu_d = [din("wgu1", [FC, 128, 2 * DC * 128]), din("wgu2", [FC, 128, 2 * DC * 128])]
    wdn_d = [din("wdn1", [DC, 128, FC * 128]), din("wdn2", [DC, 128, FC * 128])]
    wab_d = din("wab", [DC, 128, 2 * DC * 128])
    wk_d = din("wk", [DC, 128, DC * 128])
    wq_d = din("wq", [DC, 128, DC * 128])
    wmq_d = din("wmq", [DC, 128, DC * 128])
    wv_d = din("wv", [4, 128, DC * 256])
    wkvv_d = din("wkvv", [4, 128, DC * 256])
    wkvk_d = din("wkvk", [DC, 128, DC * 128])
    wcg_d = din("wcg", [DC, 128, 2 * DC * 128])
    wdg_d = din("wdg", [DC, 128, 2 * DC * 128])
    wmg_d = din("wmg", [DC, 128, 2 * DC * 128])
    wo_d = din("wo", [DC, 128, DC * 128])
    out_d = nc.dram_tensor("outT", [NB, 128, DC * S], F32, kind="ExternalOutput").ap()
    hsp_d = nc.dram_tensor("hspill", [128, DC * S], BF16, kind="Internal").ap()
    hsp_v = hsp_d.rearrange("p (c s) -> p c s", c=DC)
    wc_src = {"wmq": (wmq_d, 1024), "wmg": (wmg_d, 2048), "wq": (wq_d, 1024), "wdg": (wdg_d, 2048),
              "wo": (wo_d, 1024), "wcg": (wcg_d, 2048)}
    wc_d = {nm: nc.dram_tensor("wc_" + nm, [DC, 128, n], BF16, kind="Internal").ap() for nm, (_, n) in wc_src.items()}

    stack = ExitStack()
    with stack:
        k = KB(nc, stack)

        def sb(name, shape, dt):
            return stack.enter_context(nc.sbuf_tensor(name, list(shape), dt))

        xT = sb("xT_sb", [128, DC, S], F32)
        xT_t = [[Tile() for _ in range(NT)] for _ in range(DC)]
        cst = sb("cst_sb", [128, 512], BF16)
        cst_t = Tile()
        ones = cst[:, 0:128]
        onesblk = cst[:, 128:256]
        pswap = cst[:, 256:384]
        ident = cst[:, 384:512]
        rope = sb("rope_sb", [128, 2 * S], BF16)
        rope_t = Tile()
        prm = sb("prm_sb", [128, P_LAM], F32)
        prm_t = Tile()
        drv = sb("drv", [128, NDRV], F32)
        drv_t = Tile()
        WSLOT = 3072
        wreg = sb("wreg", [128, 3 * WSLOT], BF16)
        wslots = Ring([(wreg[:, i * WSLOT:(i + 1) * WSLOT], Tile()) for i in range(3)])
        wslots9 = Ring([(wreg[:, i * 1024:(i + 1) * 1024], Tile()) for i in range(9)])
        sqr = Ring([(sb(f"sq{i}", [128, 512], BF16), Tile()) for i in range(2)])
        f32r = Ring([(sb(f"f32t{i}", [128, 512], F32), Tile()) for i in range(4)])
        lnm, lnm_t = sb("lnm", [128, 512], F32), Tile()
        lnv, lnv_t = sb("lnv", [128, 512], F32), Tile()
        hsp_t = [Tile() for _ in range(NT)]
        bfr = Ring([(sb(f"bft{i}", [128, 512], BF16), Tile()) for i in range(5)])
        AR_WORDS = 96 * 256
        arena = sb("arena", [128, AR_WORDS], F32)

        def carve(off_kib, n_elems, dt, pat=None, **kw):
            off = int(round(off_kib * 256))
            if dt == F32:
                ap = arena[:, off:off + n_elems]
            else:
                assert n_elems % 2 == 0
                ap = arena[:, off:off + n_elems // 2].bitcast(BF16)
            if pat:
                ap = ap.rearrange(pat, **kw)
            return ap

        psb = [(stack.enter_context(nc.psum_tensor(f"ps{i}", [128, 512], F32)), Tile()) for i in range(8)]
        ps_all = Ring(psb)

        k.dma("pool", cst[:], cst_d[:, :], writes=[cst_t])
        k.dma("pool", rope[:], rope_d[:, :], writes=[rope_t])
        k.dma("sp", prm[:], prm_d[:, 0:P_LAM], writes=[prm_t])
        k.ts(drv[:, 0:32], prm[:, 0:32], 32.0, None, ALU.mult, None, [prm_t], [drv_t])
        k.ts(drv[:, V_QN:V_QN + 2], prm[:, P_QN:P_QN + 2], 8.0, None, ALU.mult, None, [prm_t], [drv_t])
        k.ts(drv[:, V_SUB:V_SUB + 1], prm[:, P_SUB:P_SUB + 1], math.sqrt(128.0) * (1.0 - LAM_INIT), None, ALU.mult, None,
             [prm_t], [drv_t])
        k.ts(drv[:, V_MQN:V_MQN + 4], prm[:, P_MQN:P_MQN + 4], 16.0, None, ALU.mult, None, [prm_t], [drv_t])
        lp_ap, lp_t = f32r.next()
        k.dma("sp", lp_ap[:, 0:256], prm_d[:, P_LAM:P_LAM + 256], writes=[lp_t])
        lt_ap, lt_t = f32r.next()
        k.tt(lt_ap[:, 0:64], lp_ap[:, 0:64], lp_ap[:, 64:128], ALU.mult, [lp_t], [lt_t])
        k.tt(lt_ap[:, 64:128], lp_ap[:, 128:192], lp_ap[:, 192:256], ALU.mult, [lp_t], [lt_t])
        k.op("dve", lambda e: e.reduce_sum(out=drv[:, V_TMP:V_TMP + 1], in_=lt_ap[:, 0:64], axis=AX.X), [lt_t], [drv_t])
        k.op("dve", lambda e: e.reduce_sum(out=drv[:, V_TMP + 1:V_TMP + 2], in_=lt_ap[:, 64:128], axis=AX.X), [lt_t], [drv_t])
        k.act(drv[:, V_TMP + 2:V_TMP + 4], drv[:, V_TMP:V_TMP + 2], AF.Exp, [drv_t], [drv_t])
        k.tt(drv[:, V_TMP + 4:V_TMP + 5], drv[:, V_TMP + 3:V_TMP + 4], drv[:, V_TMP + 2:V_TMP + 3], ALU.subtract, [drv_t], [drv_t])
        k.ts(drv[:, V_NLAM:V_NLAM + 1], drv[:, V_TMP + 4:V_TMP + 5], -LAM_INIT, None, ALU.add, None, [drv_t], [drv_t])
        CONSTS = [cst_t, rope_t, prm_t, drv_t]

        def wload(src_ap, n_elems):
            ap, t = wslots.next()
            k.dma("pool", ap[:, 0:n_elems], src_ap, writes=[t])
            return ap, t

        wc_t = {nm: [Tile() for _ in range(DC)] for nm in wc_d}

        def wload_c(nm, j, piece=0):
            ap, t = wslots9.next()
            k.dma("sp", ap[:, 0:1024], wc_d[nm][j][:, piece * 1024:(piece + 1) * 1024], reads=[wc_t[nm][j]], writes=[t])
            return ap, t

        for nm, (src, n) in wc_src.items():
            for j in range(DC):
                ap, t = wload(src[j], n)
                k.dma("sp", wc_d[nm][j], ap[:, 0:n], reads=[t], writes=[wc_t[nm][j]])
        k.barrier()

        def rms_tile(src_fn, src_tiles, nch, ncol, eps_n, ps_pick):
            ps, pst = ps_pick()
            for c in range(nch):
                sq, sqt = sqr.next()
                k.act(sq[:, 0:ncol], src_fn(c), AF.Square, [src_tiles[c]], [sqt])
                k.mm(ps[:, 0:ncol], pst, ones, sq[:, 0:ncol], c == 0, c == nch - 1, [sqt, cst_t], force=True)
            rs, rst = f32r.next()
            k.rstd(rs[:, 0:ncol], ps[:, 0:ncol], eps_n, [pst], rst)
            return rs, rst

        def norm_full(hT, hT_t, gcol, spill):
            for t in range(NT):
                sl = slice(t * TT, (t + 1) * TT)
                rs, rst = rms_tile(lambda c: xT[:, c, sl], [xT_t[c][t] for c in range(DC)], DC, TT, D * EPS, ps_all.next)
                for c in range(DC):
                    k.stt(hT[:, c, sl], xT[:, c, sl], drv[:, gcol + c:gcol + c + 1], rs[:, 0:TT], ALU.mult, ALU.mult,
                          [xT_t[c][t], rst, drv_t], [hT_t[c][t]])
                if spill:
                    k.dma("sp", hsp_v[:, :, sl], hT[:, :, sl], reads=[hT_t[c][t] for c in range(DC)], writes=[hsp_t[t]])

        def ffn(idx, gcol):
            hT = carve(0, DC * S, BF16, "p (c t) -> p c t", c=DC)
            hT_t = [[Tile() for _ in range(NT)] for _ in range(DC)]
            actT = carve(32 * S / 2048, FC * SH, BF16, "p (c t) -> p c t", c=FC)
            act_t = [[Tile() for _ in range(NTH)] for _ in range(FC)]
            norm_full(hT, hT_t, gcol, False)
            for half in range(2):
                for j in range(FC):
                    w, wt = wload(wgu_d[idx][j], 2 * DC * 128)
                    wv = w[:, 0:2 * DC * 128].rearrange("p (g c m) -> p g c m", g=2, c=DC)
                    pg = [ps_all.next() for _ in range(NTH)]
                    pu = [ps_all.next() for _ in range(NTH)]
                    for g, pp in ((0, pg), (1, pu)):
                        for c in range(DC):
                            for tl in range(NTH):
                                t = half * NTH + tl
                                k.mm(pp[tl][0][:, :], pp[tl][1], wv[:, g, c, :], hT[:, c, t * TT:(t + 1) * TT],
                                     c == 0, c == DC - 1, [wt, hT_t[c][t]])
                    for tl in range(NTH):
                        sg, sgt = bfr.next()
                        k.act(sg[:, :], pg[tl][0][:, :], AF.Silu, [pg[tl][1]], [sgt])
                        k.tt(actT[:, j, tl * TT:(tl + 1) * TT], pu[tl][0][:, :], sg[:, :], ALU.mult,
                             [pu[tl][1], sgt], [act_t[j][tl]])
                for c in range(DC):
                    w, wt = wload(wdn_d[idx][c], FC * 128)
                    wv = w[:, 0:FC * 128].rearrange("p (f m) -> p f m", f=FC)
                    pd = [ps_all.next() for _ in range(NTH)]
                    for f in range(FC):
                        for tl in range(NTH):
                            k.mm(pd[tl][0][:, :], pd[tl][1], wv[:, f, :], actT[:, f, tl * TT:(tl + 1) * TT],
                                 f == 0, f == FC - 1, [wt, act_t[f][tl]])
                    for tl in range(NTH):
                        t = half * NTH + tl
                        sl = slice(t * TT, (t + 1) * TT)
                        k.stt(xT[:, c, sl], pd[tl][0][:, :], 0.5, xT[:, c, sl], ALU.mult, ALU.add,
                              [pd[tl][1], xT_t[c][t]], [xT_t[c][t]])
            k.barrier()

        def gated_proj(w_nm, jc, y_rhs_fn, y_reads_fn, h_rhs_fn, h_reads_fn, bcol, ncol, emit):
            wg, wgt = wload_c(w_nm, jc, 1)
            wy, wyt = wload_c(w_nm, jc, 0)
            wgv = wg[:, 0:DC * 128].rearrange("p (c m) -> p c m", c=DC)
            wyv = wy[:, 0:DC * 128].rearrange("p (c m) -> p c m", c=DC)
            py, pyt = ps_all.next()
            pg, pgt = ps_all.next()
            for c in range(DC):
                k.mm(pg[:, 0:ncol], pgt, wgv[:, c, :], h_rhs_fn(c), c == 0, c == DC - 1, [wgt] + h_reads_fn(c))
            for c in range(DC):
                k.mm(py[:, 0:ncol], pyt, wyv[:, c, :], y_rhs_fn(c), c == 0, c == DC - 1, [wyt] + y_reads_fn(c))
            g, gt = f32r.next()
            k.act(g[:, 0:ncol], pg[:, 0:ncol], AF.Sigmoid, [pgt, prm_t], [gt], bias=prm[:, P_BG + bcol:P_BG + bcol + 1])
            emit(py, pyt, g, gt)

        def wo_apply(m_fn, m_reads_fn, t0, ncol, xtiles_fn):
            for oc in range(DC):
                w, wt = wload_c("wo", oc)
                wv = w[:, 0:DC * 128].rearrange("p (c m) -> p c m", c=DC)
                po, pot = ps_all.next()
                for c in range(DC):
                    k.mm(po[:, 0:ncol], pot, wv[:, c, :], m_fn(c), c == 0, c == DC - 1, [wt] + m_reads_fn(c))
                xt = xtiles_fn(oc)
                k.tt(xT[:, oc, t0:t0 + ncol], po[:, 0:ncol], xT[:, oc, t0:t0 + ncol], ALU.add, [pot, xt], [xt])

        def memkv(b, mkT, mk_t, mv, mv_t):
            memT = carve(32, DC * MEM, F32, "p (c t) -> p c t", c=DC)
            memT_t = [Tile() for _ in range(DC)]
            memn = carve(40, DC * MEM, BF16, "p (c t) -> p c t", c=DC)
            memn_t = [Tile() for _ in range(DC)]
            for c in range(DC):
                k.dma("sp", memT[:, c, :], memT_d[b][:, c * MEM:(c + 1) * MEM], writes=[memT_t[c]])
            rs, rst = rms_tile(lambda c: memT[:, c, :], memT_t, DC, MEM, D * EPS, ps_all.next)
            for c in range(DC):
                k.stt(memn[:, c, :], memT[:, c, :], drv[:, V_MEMN + c:V_MEMN + c + 1], rs[:, 0:MEM], ALU.mult, ALU.mult,
                      [memT_t[c], rst, drv_t], [memn_t[c]])
            for hm in range(4):
                pk = []
                for i in range(2):
                    fc = 2 * hm + i
                    w, wt = wload(wkvk_d[fc], DC * 128)
                    wv = w[:, 0:DC * 128].rearrange("p (c m) -> p c m", c=DC)
                    p, pt = ps_all.next()
                    for c in range(DC):
                        k.mm(p[:, 0:MEM], pt, wv[:, c, :], memn[:, c, :], c == 0, c == DC - 1, [wt, memn_t[c]])
                    pk.append((p, pt))
                rs, rst = rms_tile(lambda i: pk[i][0][:, 0:MEM], [pk[0][1], pk[1][1]], 2, MEM, 256 * EPS, ps_all.next)
                for i in range(2):
                    fc = 2 * hm + i
                    k.stt(mkT[:, fc, :], pk[i][0][:, 0:MEM], drv[:, V_MKN + i:V_MKN + i + 1], rs[:, 0:MEM], ALU.mult, ALU.mult,
                          [pk[i][1], rst, drv_t], [mk_t[fc]])
            for qd in range(4):
                w, wt = wload(wkvv_d[qd], DC * 256)
                wv = w[:, 0:DC * 256].rearrange("p (c n) -> p c n", c=DC)
                for tk in range(2):
                    p, pt = ps_all.next()
                    for c in range(DC):
                        k.mm(p[:, 0:256], pt, memn[:, c, tk * 128:(tk + 1) * 128], wv[:, c, :], c == 0, c == DC - 1,
                             [wt, memn_t[c]])
                    k.act(mv[:, tk, qd * 256:(qd + 1) * 256], p[:, 0:256], AF.Copy, [pt], [mv_t[tk]])

        def qk_partA(px, pxt, gcol, ncol):
            y, yt = bfr.next()
            k.act(y[:, 0:ncol], px[:, 0:ncol], AF.Identity, [pxt, drv_t], [yt], scale=drv[:, gcol:gcol + 1])
            sq, sqt = sqr.next()
            k.act(sq[:, 0:ncol], px[:, 0:ncol], AF.Square, [pxt], [sqt])
            return (y, yt, sq, sqt)

        def qk_partB(stA, out_ap, out_t, t0, ncol):
            y, yt, sq, sqt = stA
            pss, psst = ps_all.next()
            k.mm(pss[:, 0:ncol], psst, onesblk, sq[:, 0:ncol], True, True, [sqt, cst_t])
            psp, pspt = ps_all.next()
            k.mm(psp[:, 0:ncol], pspt, pswap, y[:, 0:ncol], True, True, [yt, cst_t])
            rs, rst = f32r.next()
            k.rstd(rs[:, 0:ncol], pss[:, 0:ncol], 64 * EPS, [psst], rst)
            t1, t1t = f32r.next()
            k.tt(t1[:, 0:ncol], y[:, 0:ncol], rope[:, t0:t0 + ncol], ALU.mult, [yt, rope_t], [t1t])
            t2, t2t = f32r.next()
            k.tt(t2[:, 0:ncol], psp[:, 0:ncol], rope[:, S + t0:S + t0 + ncol], ALU.mult, [pspt, rope_t], [t2t])
            k.tt(t1[:, 0:ncol], t1[:, 0:ncol], t2[:, 0:ncol], ALU.add, [t1t, t2t], [t1t])
            if isinstance(out_ap, list):
                for prow, dst in out_ap:
                    k.tt(dst, t1[prow, 0:ncol], rs[prow, 0:ncol], ALU.mult, [t1t, rst], [out_t])
            else:
                k.tt(out_ap, t1[:, 0:ncol], rs[:, 0:ncol], ALU.mult, [t1t, rst], [out_t])

        mkT = carve(88, DC * MEM, BF16, "p (c t) -> p c t", c=DC)
        mk_t = [Tile() for _ in range(DC)]
        mv = carve(92, 2 * D, BF16, "p (k n) -> p k n", k=2)
        mv_t = [Tile() for _ in range(2)]

        def mixer(b):
            sc = S / 2048.0
            hT = carve(0, DC * S, BF16, "p (c t) -> p c t", c=DC)
            hT_t = [[Tile() for _ in range(NT)] for _ in range(DC)]
            norm_full(hT, hT_t, V_MIX, True)

            if "conv" in stages:
                cT = carve(32 * sc, DC * S, BF16, "p (c t) -> p c t", c=DC)
                cT_t = [[Tile() for _ in range(NT)] for _ in range(DC)]
                mT = carve(64 * sc, DC * TT, BF16, "p (c t) -> p c t", c=DC)
                mT_t = [Tile() for _ in range(DC)]
                upad = carve(64 * sc + 8, S + 2 * PAD + 2, BF16)
                upad_t = Tile()
                dg = carve(64 * sc + 8 + (S + 32) * 2 / 1024.0 + 0.05, CW * 128, BF16, "p (j m) -> p j m", j=CW)
                dg_t = Tile()
                k.op("dve", lambda e: e.memset(upad[:, 0:S + 2 * PAD + 2], 0.0), [], [upad_t])
                for c in range(DC):
                    w, wt = wload(wab_d[c], 2 * DC * 128)
                    wv = w[:, 0:2 * DC * 128].rearrange("p (g c m) -> p g c m", g=2, c=DC)
                    for j in range(CW):
                        k.ts(dg[:, j, :], ident, prm[:, P_CW + c * CW + j:P_CW + c * CW + j + 1], None, ALU.mult, None,
                             [cst_t, prm_t], [dg_t])
                    for t in range(NT):
                        sl = slice(t * TT, (t + 1) * TT)
                        pa, pat = ps_all.next()
                        pb, pbt = ps_all.next()
                        for g, (pp, ppt) in ((0, (pa, pat)), (1, (pb, pbt))):
                            for cc in range(DC):
                                k.mm(pp[:, :], ppt, wv[:, g, cc, :], hT[:, cc, sl], cc == 0, cc == DC - 1, [wt, hT_t[cc][t]])
                        sg, sgt = bfr.next()
                        k.act(sg[:, :], pb[:, :], AF.Sigmoid, [pbt], [sgt])
                        k.tt(upad[:, PAD + t * TT:PAD + (t + 1) * TT], pa[:, :], sg[:, :], ALU.mult, [pat, sgt], [upad_t])
                    for t in range(NT):
                        pc, pct = ps_all.next()
                        for j in range(CW):
                            k.mm(pc[:, :], pct, dg[:, j, :], upad[:, t * TT + j:t * TT + j + TT], j == 0, j == CW - 1,
                                 [dg_t, upad_t])
                        k.act(cT[:, c, t * TT:(t + 1) * TT], pc[:, :], AF.Identity, [pct, prm_t], [cT_t[c][t]],
                              bias=prm[:, P_CB + c:P_CB + c + 1])
                k.barrier()
                def conv_ln(t):
                    sl = slice(t * TT, (t + 1) * TT)
                    pm, pmt = ps_all.next()
                    pq, pqt = ps_all.next()
                    for c in range(DC):
                        sq, sqt = sqr.next()
                        k.act(sq[:, :], cT[:, c, sl], AF.Square, [cT_t[c][t]], [sqt])
                        k.mm(pm[:, :], pmt, ones, cT[:, c, sl], c == 0, c == DC - 1, [cT_t[c][t], cst_t])
                        k.mm(pq[:, :], pqt, ones, sq[:, :], c == 0, c == DC - 1, [sqt, cst_t], force=True)
                    mean, meant = lnm, lnm_t
                    k.ts(mean[:, :], pm[:, :], 1.0 / D, None, ALU.mult, None, [pmt], [meant])
                    var, vart = lnv, lnv_t
                    k.tt(var[:, :], mean[:, :], mean[:, :], ALU.mult, [meant], [vart])
                    k.stt(var[:, :], pq[:, :], 1.0 / D, var[:, :], ALU.mult, ALU.subtract, [pqt, vart], [vart])
                    k.rstd(var[:, :], var[:, :], EPS, [vart], vart)
                    k.stt(mean[:, :], mean[:, :], -1.0, var[:, :], ALU.mult, ALU.mult, [meant, vart], [meant])
                    for c in range(DC):
                        tmp, tmpt = f32r.next()
                        k.tt(tmp[:, :], cT[:, c, sl], var[:, :], ALU.mult, [cT_t[c][t], vart], [tmpt])
                        k.tt(tmp[:, :], tmp[:, :], mean[:, :], ALU.add, [tmpt, meant], [tmpt])
                        k.act(cT[:, c, sl], tmp[:, :], AF.Silu, [tmpt, prm_t], [cT_t[c][t]],
                              bias=prm[:, P_LNB + c:P_LNB + c + 1], scale=prm[:, P_LNG + c:P_LNG + c + 1])

                def conv_proj(t):
                    sl = slice(t * TT, (t + 1) * TT)
                    for jc in range(DC):
                        def emit(py, pyt, g, gt, jc=jc):
                            k.tt(mT[:, jc, :], py[:, :], g[:, :], ALU.mult, [pyt, gt], [mT_t[jc]])
                        gated_proj("wcg", jc, lambda c: cT[:, c, sl], lambda c: [cT_t[c][t]],
                                   lambda c: hT[:, c, sl], lambda c: [hT_t[c][t]], jc, TT, emit)
                    wo_apply(lambda c: mT[:, c, :], lambda c: [mT_t[c]], t * TT, TT, lambda oc: xT_t[oc][t])

                conv_ln(0)
                for t in range(NT):
                    if t + 1 < NT:
                        conv_ln(t + 1)
                    conv_proj(t)
                k.barrier()

            if "att" in stages:
                k.barrier()
                kT = carve(0, DC * S, BF16, "p (c t) -> p c t", c=DC)
                kT_t = [[Tile() for _ in range(NT)] for _ in range(DC)]
                V = carve(32 * sc, NK * D, BF16, "p (k n) -> p k n", k=NK)
                V_t = [Tile() for _ in range(NK)]
                hbuf = [(carve(64 * sc + 8 * i, DC * TT, BF16, "p (c t) -> p c t", c=DC), Tile()) for i in range(2)]
                for t in range(NT):
                    hb, hbt = hbuf[t % 2]
                    k.dma("sp", hb[:, :, :], hsp_v[:, :, t * TT:(t + 1) * TT], reads=[hsp_t[t]], writes=[hbt])
                    prevB = None
                    for kc in range(DC):
                        w, wt = wload(wk_d[kc], DC * 128)
                        wv = w[:, 0:DC * 128].rearrange("p (c m) -> p c m", c=DC)
                        px, pxt = ps_all.next()
                        for c in range(DC):
                            k.mm(px[:, :], pxt, wv[:, c, :], hb[:, c, :], c == 0, c == DC - 1, [wt, hbt])
                        stA = qk_partA(px, pxt, V_KN, TT)
                        if prevB is not None:
                            qk_partB(*prevB)
                        prevB = (stA, kT[:, kc, t * TT:(t + 1) * TT], kT_t[kc][t], t * TT, TT)
                    qk_partB(*prevB)
                    for qd in range(4):
                        w, wt = wload(wv_d[qd], DC * 256)
                        wv = w[:, 0:DC * 256].rearrange("p (c n) -> p c n", c=DC)
                        for tk in range(4):
                            kk = t * 4 + tk
                            p, pt = ps_all.next()
                            for c in range(DC):
                                k.mm(p[:, 0:256], pt, hb[:, c, tk * 128:(tk + 1) * 128], wv[:, c, :], c == 0, c == DC - 1,
                                     [wt, hbt])
                            if (qd + tk) % 2 == 0:
                                k.act(V[:, kk, qd * 256:(qd + 1) * 256], p[:, 0:256], AF.Copy, [pt], [V_t[kk]])
                            else:
                                k.op("dve", lambda e, kk=kk, qd=qd, p=p: e.tensor_copy(out=V[:, kk, qd * 256:(qd + 1) * 256],
                                                                                        in_=p[:, 0:256]), [pt], [V_t[kk]])
                k.barrier()
                hq = [(carve(64 * sc + 4 * i, DC * QT, BF16, "p (c t) -> p c t", c=DC), Tile()) for i in range(2)]
                Qb = carve(64 * sc + 8, DC * 2 * QT, BF16, "p (c t) -> p c t", c=DC)
                Qb_t = [Tile() for _ in range(DC)]
                oT = carve(64 * sc + 16, DC * QT, BF16, "p (c t) -> p c t", c=DC)
                oT_t = [Tile() for _ in range(DC)]
                mg = carve(64 * sc + 20, DC * QT, BF16, "p (c t) -> p c t", c=DC)
                mg_t = [Tile() for _ in range(DC)]
                mqT, mq_t = mg, mg_t
                acc, acc_t = lnm, lnm_t
                ps_lo = Ring(psb[0:4])
                ps_hi = Ring(psb[6:8])
                for c in range(DC):
                    k.op("dve", lambda e, c=c: e.memset(Qb[:, c, :], 0.0), [], [Qb_t[c]])
                def hq_load(qj):
                    hbj, hbtj = hq[qj % 2]
                    k.dma("sp", hbj[:, :, :], hsp_v[:, :, qj * QT:(qj + 1) * QT], reads=[hsp_t[qj * QT // TT]], writes=[hbtj])

                for qi in range(NQ):
                    q0 = qi * QT
                    tq = q0 // TT
                    hb, hbt = hq[qi % 2]
                    if qi == 0:
                        hq_load(0)
                    mst = [dict() for _ in range(4)]

                    def m1a(hm):
                        pk = []
                        for i in range(2):
                            fc = 2 * hm + i
                            w, wt = wload_c("wmq", fc)
                            wv = w[:, 0:DC * 128].rearrange("p (c m) -> p c m", c=DC)
                            p, pt = ps_all.next()
                            for c in range(DC):
                                k.mm(p[:, 0:QT], pt, wv[:, c, :], hb[:, c, :], c == 0, c == DC - 1, [wt, hbt])
                            pk.append((p, pt))
                        mst[hm]["pk"] = pk

                    def m1b(hm):
                        pk = mst[hm]["pk"]
                        rs, rst = rms_tile(lambda i: pk[i][0][:, 0:QT], [pk[0][1], pk[1][1]], 2, QT, 256 * EPS, ps_all.next)
                        for i in range(2):
                            fc = 2 * hm + i
                            k.stt(mqT[:, fc, :], pk[i][0][:, 0:QT], drv[:, V_MQN + i:V_MQN + i + 1], rs[:, 0:QT],
                                  ALU.mult, ALU.mult, [pk[i][1], rst, drv_t], [mq_t[fc]])

                    def m2(hm):
                        psc, psct = ps_all.next()
                        for mc in range(2):
                            for i in range(2):
                                fc = 2 * hm + i
                                k.mm(psc[:, mc * QT:(mc + 1) * QT], psct, mkT[:, fc, mc * 128:(mc + 1) * 128], mqT[:, fc, :],
                                     i == 0, i == 1, [mk_t[fc], mq_t[fc]])
                        pm_, pmt_ = bfr.next()
                        k.act(pm_[:, :], psc[:, :], AF.Exp, [psct], [pmt_], scale=1.0 / 16.0)
                        mst[hm]["pm"] = (pm_, pmt_)

                    def m3(hm):
                        pm_, pmt_ = mst[hm]["pm"]
                        pso = [ps_all.next() for _ in range(2)]
                        pss, psst = ps_all.next()
                        for e2 in range(2):
                            for mc in range(2):
                                k.mm(pso[e2][0][:, 0:QT], pso[e2][1], mv[:, mc, hm * 256 + e2 * 128:hm * 256 + (e2 + 1) * 128],
                                     pm_[:, mc * QT:(mc + 1) * QT], mc == 0, mc == 1, [mv_t[mc], pmt_])
                        for mc in range(2):
                            k.mm(pss[:, 0:QT], psst, ones, pm_[:, mc * QT:(mc + 1) * QT], mc == 0, mc == 1, [pmt_, cst_t])
                        rr, rrt = f32r.next()
                        k.recip(rr[:, 0:QT], pss[:, 0:QT], [psst], rrt)
                        for e2 in range(2):
                            fc = 2 * hm + e2
                            k.tt(oT[:, fc, :], pso[e2][0][:, 0:QT], rr[:, 0:QT], ALU.mult, [pso[e2][1], rrt], [oT_t[fc]])

                    for fn, hm in ((m1a, 0), (m1a, 1), (m1b, 0), (m1a, 2), (m1b, 1), (m2, 0), (m1a, 3), (m1b, 2), (m2, 1),
                                   (m3, 0), (m1b, 3), (m2, 2), (m3, 1), (m2, 3), (m3, 2), (m3, 3)):
                        fn(hm)
                    for jc in range(DC):
                        def emit(py, pyt, g, gt, jc=jc):
                            k.tt(mg[:, jc, :], py[:, 0:QT], g[:, 0:QT], ALU.mult, [pyt, gt], [mg_t[jc]])
                        gated_proj("wmg", jc, lambda c: oT[:, c, :], lambda c: [oT_t[c]],
                                   lambda c: hb[:, c, :], lambda c: [hbt], 16 + jc, QT, emit)
                    prevB = None
                    for qc in range(DC):
                        w, wt = wload_c("wq", qc)
                        wv = w[:, 0:DC * 128].rearrange("p (c m) -> p c m", c=DC)
                        px, pxt = ps_all.next()
                        for c in range(DC):
                            k.mm(px[:, 0:QT], pxt, wv[:, c, :], hb[:, c, :], c == 0, c == DC - 1, [wt, hbt])
                        stA = qk_partA(px, pxt, V_QN, QT)
                        if prevB is not None:
                            qk_partB(*prevB)
                        prevB = (stA, [(slice(0, 64), Qb[0:64, qc, 0:QT]), (slice(64, 128), Qb[64:128, qc, QT:2 * QT])],
                                 Qb_t[qc], q0, QT)
                    qk_partB(*prevB)
                    def epilogue_stages(h, po, pot, acc, acc_t):
                        st = {}

                        def s1():
                            st["accb"] = bfr.next()
                            k.act(st["accb"][0][:, :], acc[:, :], AF.Copy, [acc_t], [st["accb"][1]])

                        def s1b():
                            st["psm"] = ps_hi.next()
                            k.mm(st["psm"][0][:, :], st["psm"][1], ones, st["accb"][0][:, :], True, True, [st["accb"][1], cst_t])

                        def s2():
                            st["rr"] = f32r.items[0]
                            k.recip(st["rr"][0][:, :], st["psm"][0][:, :], [st["psm"][1]], st["rr"][1])

                        def s3():
                            rr, rrt = st["rr"]
                            k.tt(rr[:, :], po[:, :], rr[:, :], ALU.mult, [pot, rrt], [rrt])
                            st["r0"] = f32r.items[1]
                            r0, r0t = st["r0"]
                            k.stt(r0[:, 0:QT], rr[:, QT:2 * QT], drv[:, V_NLAM:V_NLAM + 1], rr[:, 0:QT], ALU.mult, ALU.add,
                                  [rrt, drv_t], [r0t])

                        def s4():
                            r0, r0t = st["r0"]
                            st["sq"] = sqr.next()
                            k.act(st["sq"][0][:, 0:QT], r0[:, 0:QT], AF.Square, [r0t], [st["sq"][1]])

                        def s4b():
                            sq, sqt = st["sq"]
                            st["pn"] = ps_hi.next()
                            k.mm(st["pn"][0][:, 0:QT], st["pn"][1], ones, sq[:, 0:QT], True, True, [sqt, cst_t])

                        def s5():
                            r0, r0t = st["r0"]
                            k.rstd(r0[:, QT:2 * QT], st["pn"][0][:, 0:QT], 128 * EPS, [st["pn"][1]], r0t)

                        def s6():
                            r0, r0t = st["r0"]
                            k.stt(oT[:, h, :], r0[:, 0:QT], drv[:, V_SUB:V_SUB + 1], r0[:, QT:2 * QT], ALU.mult, ALU.mult,
                                  [r0t, drv_t], [oT_t[h]])

                        return [s1, s1b, s2, s3, s4, s4b, s5, s6]

                    if qi + 1 < NQ:
                        hq_load(qi + 1)
                    pending_epi = []
                    for h in range(DC):
                        po, pot = psb[4 + (h % 2)]
                        acc, acc_t = (lnm, lnm_t) if h % 2 == 0 else (lnv, lnv_t)
                        pend = []

                        def pv_step(kc, pe0, pet0):
                            k.mm(po[:, :], pot, V[:, kc, h * 128:(h + 1) * 128], pe0[:, :], kc == 0, kc == NK - 1, [V_t[kc], pet0])
                            if kc == 0:
                                k.op("dve", lambda e: e.tensor_copy(out=acc[:, :], in_=pe0[:, :]), [pet0], [acc_t])
                            else:
                                k.tt(acc[:, :], acc[:, :], pe0[:, :], ALU.add, [acc_t, pet0], [acc_t])

                        for kc in range(NK):
                            psc, psct = ps_lo.next()
                            k.mm(psc[:, :], psct, kT[:, h, kc * 128:(kc + 1) * 128], Qb[:, h, :], True, True,
                                 [kT_t[h][kc // 4], Qb_t[h]])
                            pe_, pet_ = bfr.next()
                            k.act(pe_[:, :], psc[:, :], AF.Exp, [psct], [pet_], scale=0.125)
                            pend.append((kc, pe_, pet_))
                            if len(pend) > 1:
                                pv_step(*pend.pop(0))
                            if pending_epi and kc >= 1 and (kc % 2 == 1 or NK < 16):
                                pending_epi.pop(0)()
                        pv_step(*pend.pop(0))
                        while pending_epi:
                            pending_epi.pop(0)()
                        pending_epi = epilogue_stages(h, po, pot, acc, acc_t)
                    while pending_epi:
                        pending_epi.pop(0)()
                    for jc in range(DC):
                        def emit(py, pyt, g, gt, jc=jc):
                            tmp, tmpt = f32r.next()
                            k.tt(tmp[:, 0:QT], py[:, 0:QT], g[:, 0:QT], ALU.mult, [pyt, gt], [tmpt])
                            k.tt(mg[:, jc, :], tmp[:, 0:QT], mg[:, jc, :], ALU.add, [tmpt, mg_t[jc]], [mg_t[jc]])
                        gated_proj("wdg", jc, lambda c: oT[:, c, :], lambda c: [oT_t[c]],
                                   lambda c: hb[:, c, :], lambda c: [hbt], 8 + jc, QT, emit)
                    wo_apply(lambda c: mg[:, c, :], lambda c: [mg_t[c]], q0, QT, lambda oc: xT_t[oc][tq])
                k.barrier()

        for b in range(NB):
            for c in range(DC):
                for t in range(NT):
                    k.dma("sp", xT[:, c, t * TT:(t + 1) * TT], xT_d[b][:, c * S + t * TT:c * S + (t + 1) * TT],
                          writes=[xT_t[c][t]])
            if "att" in stages:
                memkv(b, mkT, mk_t, mv, mv_t)
                k.barrier()
            if "ffn1" in stages:
                ffn(0, V_FFN1)
            if "conv" in stages or "att" in stages:
                mixer(b)
            if "ffn2" in stages:
                ffn(1, V_FFN2)
            for c in range(DC):
                for t in range(NT):
                    k.dma("sp", out_d[b][:, c * S + t * TT:c * S + (t + 1) * TT], xT[:, c, t * TT:(t + 1) * TT],
                          reads=[xT_t[c][t]])
        k.barrier()
        build_program.last_nins = k.nins
    return nc


def _colblock(W, o, n=128):
    K = W.shape[0]
    return np.ascontiguousarray(W[:, o:o + n].reshape(K // 128, 128, n).transpose(1, 0, 2))


def _vec(v):
    return np.ascontiguousarray(v.reshape(-1, 128).T)


def rope_tables(S):
    half = 32
    inv_freq = (1.0 / (np.float32(10000.0) ** (np.arange(0, 64, 2, dtype=np.float32) / np.float32(64)))).astype(np.float32)
    ang = (np.arange(S, dtype=np.float32)[:, None] * inv_freq[None, :]).astype(np.float32)
    cos = np.cos(ang).astype(np.float32)
    sin = np.sin(ang).astype(np.float32)
    p = np.arange(128)
    C = cos[:, p % half].T
    sign = np.where((p % 64) < half, -1.0, 1.0).astype(np.float32)[:, None]
    Ss = sin[:, p % half].T * sign
    return np.ascontiguousarray(np.concatenate([C, Ss], axis=1).astype(np.float32))


def const_tables():
    p = np.arange(128)
    ones = np.ones((128, 128), np.float32)
    onesblk = (p[:, None] // 64 == p[None, :] // 64).astype(np.float32)
    sw = np.where((p % 64) < 32, p + 32, p - 32)
    pswap = (p[:, None] == sw[None, :]).astype(np.float32)
    ident = np.eye(128, dtype=np.float32)
    return np.ascontiguousarray(np.concatenate([ones, onesblk, pswap, ident], axis=1))


def prep_shared(inp, S):
    f = lambda a: np.asarray(a, dtype=np.float32)
    w_in = f(inp["w_in"])[0]
    c1, c2, c3, c4, c5 = 2048, 3072, 4096, 5120, 6144
    sh = {}
    prm = np.zeros((128, NPRM), np.float32)
    prm[:, 0:8] = _vec(f(inp["ffn1_norm"])[0])
    prm[:, 8:16] = _vec(f(inp["mix_norm"])[0])
    prm[:, 16:24] = _vec(f(inp["ffn2_norm"])[0])
    prm[:, 24:32] = _vec(f(inp["mem_norm"])[0])
    prm[:, P_BG:P_BG + 24] = _vec(f(inp["b_gate"])[0])
    prm[:, P_CB:P_CB + 8] = _vec(f(inp["conv_b"])[0])
    prm[:, P_LNG:P_LNG + 8] = _vec(f(inp["conv_ln_g"])[0])
    prm[:, P_LNB:P_LNB + 8] = _vec(f(inp["conv_ln_b"])[0])
    prm[:, P_QN] = np.tile(f(inp["diff_q_norm"])[0], 2)
    prm[:, P_KN] = np.tile(f(inp["diff_k_norm"])[0], 2)
    prm[:, P_SUB] = f(inp["diff_subln"])[0]
    prm[:, P_MQN:P_MQN + 2] = _vec(f(inp["mem_q_norm"])[0])
    prm[:, P_MKN:P_MKN + 2] = _vec(f(inp["mem_k_norm"])[0])
    cw = f(inp["conv_w"])[0][:, 0, :]
    prm[:, P_CW:P_CW + 248] = cw.T.reshape(DC, 128, CW).transpose(1, 0, 2).reshape(128, DC * CW)
    prm[:, P_LAM:P_LAM + 256] = np.broadcast_to(f(inp["diff_lambda"])[0].reshape(1, 256), (128, 256))
    sh["prm"] = prm
    sh["cst"] = const_tables()
    sh["rope"] = rope_tables(S)

    def blocks(W, offs, n=128):
        return np.ascontiguousarray(np.stack([_colblock(W, o, n).reshape(128, -1) for o in offs]))

    def blocks2(Wa, oa, Wb, ob):
        out = []
        for a, b_ in zip(oa, ob):
            out.append(np.concatenate([_colblock(Wa, a).reshape(128, -1), _colblock(Wb, b_).reshape(128, -1)], axis=1))
        return np.ascontiguousarray(np.stack(out))

    r8 = [i * 128 for i in range(8)]
    for i, nm in ((1, "ffn1"), (2, "ffn2")):
        wgu = f(inp[f"{nm}_w_gu"])[0]
        wdn = f(inp[f"{nm}_w_down"])[0]
        sh[f"wgu{i}"] = blocks2(wgu, [j * 128 for j in range(FC)], wgu, [FF + j * 128 for j in range(FC)])
        sh[f"wdn{i}"] = blocks(wdn, r8)
    sh["wab"] = blocks2(w_in, r8, w_in, [1024 + o for o in r8])
    sh["wq"] = blocks(w_in, [c1 + o for o in r8])
    sh["wk"] = blocks(w_in, [c2 + o for o in r8])
    sh["wv"] = blocks(w_in, [c3 + i * 256 for i in range(4)], 256)
    sh["wmq"] = blocks(w_in, [c4 + o for o in r8])
    wkv = f(inp["w_mem_kv"])[0]
    sh["wkvk"] = blocks(wkv, r8)
    sh["wkvv"] = blocks(wkv, [1024 + i * 256 for i in range(4)], 256)
    sh["wcg"] = blocks2(f(inp["w_conv_out"])[0], r8, w_in, [c5 + o for o in r8])
    sh["wdg"] = blocks2(f(inp["w_diff_out"])[0], r8, w_in, [c5 + 1024 + o for o in r8])
    sh["wmg"] = blocks2(f(inp["w_mem_out"])[0], r8, w_in, [c5 + 2048 + o for o in r8])
    sh["wo"] = blocks(f(inp["w_o"])[0], r8)
    return sh


def to_fm(x):
    nb, T, _ = x.shape
    return np.ascontiguousarray(x.reshape(nb, T, DC, 128).transpose(0, 3, 2, 1).reshape(nb, 128, DC * T))


def from_fm(y, T):
    nb = y.shape[0]
    return np.ascontiguousarray(y.reshape(nb, 128, DC, T).transpose(0, 3, 2, 1).reshape(nb, T, D))


_PROG_CACHE = {}


def run(inputs, S, NB, ncores, stages=("ffn1", "conv", "att", "ffn2")):
    x = np.asarray(inputs["x"], dtype=np.float32)
    mem = np.asarray(inputs["mem"], dtype=np.float32)
    sh = prep_shared(inputs, S)
    key = (S, NB, tuple(stages))
    if key not in _PROG_CACHE:
        _PROG_CACHE[key] = build_program(S, NB, stages)
    nc = _PROG_CACHE[key]
    in_maps = []
    for i in range(ncores):
        m = dict(sh)
        m["xT"] = to_fm(x[i * NB:(i + 1) * NB])
        m["memT"] = to_fm(mem[i * NB:(i + 1) * NB])
        in_maps.append(m)
    res = run_bass_kernel_spmd(nc, in_maps, core_ids=list(range(ncores)))
    outs = [from_fm(np.asarray(r["outT"]), S) for r in res.results]
    return np.concatenate(outs, axis=0)


def kernel(**inputs):
    B, S, _ = inputs["x"].shape
    NB = B // NCORES
    return run(inputs, S, NB, NCORES).astype(np.float32)
```
